# Optimizing a Trainium2 kernel written in Bass

```python
import math
import jax, jax.numpy as jnp
from jax import lax
import numpy as np

D_MODEL = 1024
BATCH = 16
SEQ = 2048
DEPTH = 2

GRID_W = 64
CTX_LEN = 256
HA = 4
DA = 64
HB = 8
NOPE_B = 64
ROPE_B = 32
V_B = 64
Q_RANK = 256
KV_RANK = 128
HC = 8
KVC = 2
G_C = HC // KVC
DC = 64
WINDOW = 128
BRANCH_W = 512
N_BRANCH = 3
Q_BLOCK = 128
D_FF = 2816
CONV_W = 3
ROPE_BASE = 10000.0
EPS = 1e-6
NEG_INF = -1e30
N_MOD = 6
MLA_SCALE = (NOPE_B + ROPE_B) ** -0.5

IN_WIDTHS = (HA * 2 * DA, HA * 2 * DA, HA * 2 * DA, Q_RANK, KV_RANK, ROPE_B, HC * DC, KVC * DC, KVC * DC)
IN_SPLITS = tuple(int(v) for v in np.cumsum(IN_WIDTHS)[:-1])
N_IN = int(sum(IN_WIDTHS))

kernel_name = "hybrid_prefix_diffusion_block"


def rms_norm(x, g):
    xf = x.astype(jnp.float32)
    y = xf * lax.rsqrt(jnp.mean(xf * xf, axis=-1, keepdims=True) + EPS)
    return (y * g.astype(jnp.float32)).astype(x.dtype)


def modulate(x, shift, scale):
    return x * (1 + scale) + shift


def axial_rope_table(n_rows, rot_dim):
    n_freq = rot_dim // 4
    inv = ROPE_BASE ** (-jnp.arange(n_freq, dtype=jnp.float32) / n_freq)
    rows = jnp.repeat(jnp.arange(n_rows, dtype=jnp.float32), GRID_W)
    cols = jnp.tile(jnp.arange(GRID_W, dtype=jnp.float32), n_rows)
    ang = jnp.concatenate([rows[:, None] * inv, cols[:, None] * inv], axis=-1)
    return jnp.cos(ang), jnp.sin(ang)


def apply_rope(x, cos, sin):
    shape = (1, x.shape[1]) + (1,) * (x.ndim - 3) + (cos.shape[-1],)
    cs = cos.reshape(shape).astype(x.dtype)
    sn = sin.reshape(shape).astype(x.dtype)
    x1, x2 = jnp.split(x, 2, axis=-1)
    return jnp.concatenate([x1 * cs - x2 * sn, x2 * cs + x1 * sn], axis=-1)


def sweep_query_blocks(fn, *qs):
    n = qs[0].shape[1]
    nb = n // Q_BLOCK
    blocks = tuple(jnp.moveaxis(q.reshape(q.shape[0], nb, Q_BLOCK, *q.shape[2:]), 1, 0) for q in qs)
    out = lax.map(lambda args: fn(*args), blocks)
    out = jnp.moveaxis(out, 0, 1)
    return out.reshape(out.shape[0], n, *out.shape[3:])


def softmax_with_sink(s, sink):
    s_all = jnp.concatenate([s, jnp.broadcast_to(sink, s.shape[:-1] + (1,))], axis=-1)
    return jax.nn.softmax(s_all, axis=-1)[..., :-1]


def diff_core(q, k, v, lam):
    s = jnp.einsum('bqhid,bkhid->bhiqk', q, k, preferred_element_type=jnp.float32) * (DA ** -0.5)
    p = jax.nn.softmax(s, axis=-1)
    w = p[:, :, 0] - lam * p[:, :, 1]
    return jnp.einsum('bhqk,bkhd->bqhd', w.astype(v.dtype), v)


def diff_attention(a_lat, a_ctx, cos, sin, lam_p, g_diff, lam_init, ctx_queries):
    def heads(q, k, v):
        b, s, _ = q.shape
        return (q.reshape(b, s, HA, 2, DA), k.reshape(b, s, HA, 2, DA), v.reshape(b, s, HA, 2 * DA))
    q, k, v = heads(*a_lat)
    cq, ck, cv = heads(*a_ctx)
    q = apply_rope(q, cos, sin)
    k = apply_rope(k, cos, sin)
    lp = lam_p.astype(jnp.float32)
    lam = jnp.exp(jnp.sum(lp[0] * lp[1])) - jnp.exp(jnp.sum(lp[2] * lp[3])) + lam_init
    kk = jnp.concatenate([ck, k], axis=1)
    vv = jnp.concatenate([cv, v], axis=1)

    def finish(o):
        b, s = o.shape[:2]
        return (rms_norm(o, g_diff) * (1.0 - lam_init)).reshape(b, s, HA * 2 * DA)
    y = finish(sweep_query_blocks(lambda qb: diff_core(qb, kk, vv, lam), q))
    yc = finish(diff_core(cq, ck, cv, lam)) if ctx_queries else None
    return y, yc


def mla_core(qn, qr, kn, kr, v):
    s = (jnp.einsum('bqhd,bkhd->bhqk', qn, kn, preferred_element_type=jnp.float32)
         + jnp.einsum('bqhd,bkd->bhqk', qr, kr, preferred_element_type=jnp.float32)) * MLA_SCALE
    p = jax.nn.softmax(s, axis=-1)
    return jnp.einsum('bhqk,bkhd->bqhd', p.astype(v.dtype), v)


def mla(b_lat, b_ctx, cos, sin, g_cq, w_uq, g_ckv, w_ukv, ctx_queries):
    def qry(c_q):
        b, s, _ = c_q.shape
        qh = (rms_norm(c_q, g_cq) @ w_uq).reshape(b, s, HB, NOPE_B + ROPE_B)
        return qh[..., :NOPE_B], qh[..., NOPE_B:]

    def kv(c_kv):
        b, s, _ = c_kv.shape
        kvh = (rms_norm(c_kv, g_ckv) @ w_ukv).reshape(b, s, HB, NOPE_B + V_B)
        return kvh[..., :NOPE_B], kvh[..., NOPE_B:]
    cq_l, ckv_l, kr_l = b_lat
    cq_c, ckv_c, kr_c = b_ctx
    kn_l, v_l = kv(ckv_l)
    kn_c, v_c = kv(ckv_c)
    kr_l = apply_rope(kr_l, cos, sin)
    kn = jnp.concatenate([kn_c, kn_l], axis=1)
    kr = jnp.concatenate([kr_c, kr_l], axis=1)
    vv = jnp.concatenate([v_c, v_l], axis=1)
    qn, qr = qry(cq_l)
    qr = apply_rope(qr, cos, sin)
    y = sweep_query_blocks(lambda a, r: mla_core(a, r, kn, kr, vv), qn, qr)
    y = y.reshape(y.shape[0], y.shape[1], HB * V_B)
    yc = None
    if ctx_queries:
        qn_c, qr_c = qry(cq_c)
        yc = mla_core(qn_c, qr_c, kn_c, kr_c, v_c)
        yc = yc.reshape(yc.shape[0], yc.shape[1], HB * V_B)
    return y, yc


def window_gqa(c_lat, c_ctx, cos, sin, sink, ctx_queries):
    def heads(q, k, v):
        b, s, _ = q.shape
        return (q.reshape(b, s, KVC, G_C, DC), k.reshape(b, s, KVC, DC), v.reshape(b, s, KVC, DC))
    q, k, v = heads(*c_lat)
    cq, ck, cv = heads(*c_ctx)
    q = apply_rope(q, cos, sin)
    k = apply_rope(k, cos, sin)
    b, s = q.shape[:2]
    nb = s // WINDOW
    n_ctx = ck.shape[1]
    scale = DC ** -0.5
    sink_b = sink.astype(jnp.float32).reshape(KVC, G_C, 1, 1)

    def band(t):
        tb = t.reshape(b, nb, WINDOW, KVC, DC)
        zeros = jnp.zeros_like(tb[:, :1])
        prev = jnp.concatenate([zeros, tb[:, :-1]], axis=1)
        nxt = jnp.concatenate([tb[:, 1:], zeros], axis=1)
        return jnp.concatenate([prev, tb, nxt], axis=2)
    qb = jnp.moveaxis(q.reshape(b, nb, WINDOW, KVC, G_C, DC), 1, 0)
    kb = jnp.moveaxis(band(k), 1, 0)
    vb = jnp.moveaxis(band(v), 1, 0)

    def block(args):
        n, qn, kn, vn = args
        qpos = n * WINDOW + jnp.arange(WINDOW)
        kpos = (n - 1) * WINDOW + jnp.arange(3 * WINDOW)
        ok = (jnp.abs(qpos[:, None] - kpos[None, :]) <= WINDOW) & (kpos[None, :] >= 0) & (kpos[None, :] < s)
        s_c = jnp.einsum('bqgrd,bkgd->bgrqk', qn, ck, preferred_element_type=jnp.float32) * scale
        s_b = jnp.einsum('bqgrd,bkgd->bgrqk', qn, kn, preferred_element_type=jnp.float32) * scale
        s_b = jnp.where(ok, s_b, NEG_INF)
        p = softmax_with_sink(jnp.concatenate([s_c, s_b], axis=-1), sink_b).astype(vn.dtype)
        return (jnp.einsum('bgrqk,bkgd->bqgrd', p[..., :n_ctx], cv)
                + jnp.einsum('bgrqk,bkgd->bqgrd', p[..., n_ctx:], vn))
    out = lax.map(block, (jnp.arange(nb), qb, kb, vb))
    y = jnp.moveaxis(out, 0, 1).reshape(b, s, HC * DC)
    yc = None
    if ctx_queries:
        s_cc = jnp.einsum('bqgrd,bkgd->bgrqk', cq, ck, preferred_element_type=jnp.float32) * scale
        p_cc = softmax_with_sink(s_cc, sink_b).astype(cv.dtype)
        yc = jnp.einsum('bgrqk,bkgd->bqgrd', p_cc, cv).reshape(b, n_ctx, HC * DC)
    return y, yc


def merge_branches(u, ys, w_branch, w_gate, b_gate, w_o):
    gates = jax.nn.sigmoid(u @ w_gate + b_gate).reshape(*u.shape[:-1], N_BRANCH, D_MODEL)
    proj = jnp.einsum('bsnc,ncd->bsnd', jnp.stack(ys, axis=-2), w_branch)
    return jnp.sum(gates * proj, axis=-2) @ w_o


def dwconv(t, w, bias):
    pad = (CONV_W - 1) // 2
    y = lax.conv_general_dilated(t, w[:, None, :].astype(t.dtype), window_strides=(1,), padding=[(pad, pad)],
                                 dimension_numbers=('NWC', 'WIO', 'NWC'), feature_group_count=t.shape[-1])
    return y + bias


def conv_ffn(t, w_up, conv_w, conv_b, w_down):
    a = dwconv(t @ w_up, conv_w, conv_b)
    gate, val = jnp.split(a, 2, axis=-1)
    return (jax.nn.silu(gate) * val) @ w_down


def setup_inputs(seed: int = 0) -> dict:
    key = jax.random.key(seed)
    ks = jax.random.split(key, 32)
    f32 = jnp.float32
    nrm = lambda k, shape: jax.random.normal(k, shape, f32)
    L, D = DEPTH, D_MODEL
    return {
        "x": nrm(ks[0], (BATCH, SEQ, D)),
        "c": nrm(ks[1], (BATCH, D)),
        "ctx": nrm(ks[2], (BATCH, CTX_LEN, D)),
        "c_ctx": nrm(ks[3], (D,)),
        "w_ada": nrm(ks[4], (L, D, N_MOD * D)) * (0.5 * D ** -0.5),
        "b_ada": nrm(ks[5], (L, N_MOD * D)) * 0.02,
        "g_norm1": 1.0 + 0.02 * nrm(ks[6], (L, D)),
        "w_in": nrm(ks[7], (L, D, N_IN)) * D ** -0.5,
        "lam": nrm(ks[8], (L, 4, DA)) * 0.1,
        "g_diff": 1.0 + 0.02 * nrm(ks[9], (L, 2 * DA)),
        "g_cq": 1.0 + 0.02 * nrm(ks[10], (L, Q_RANK)),
        "w_uq": nrm(ks[11], (L, Q_RANK, HB * (NOPE_B + ROPE_B))) * Q_RANK ** -0.5,
        "g_ckv": 1.0 + 0.02 * nrm(ks[12], (L, KV_RANK)),
        "w_ukv": nrm(ks[13], (L, KV_RANK, HB * (NOPE_B + V_B))) * KV_RANK ** -0.5,
        "sink": nrm(ks[14], (L, HC)) * 0.5,
        "w_branch": nrm(ks[15], (L, N_BRANCH, BRANCH_W, D)) * BRANCH_W ** -0.5,
        "w_gate": nrm(ks[16], (L, D, N_BRANCH * D)) * D ** -0.5,
        "b_gate": nrm(ks[17], (L, N_BRANCH * D)) * 0.02,
        "w_o": nrm(ks[18], (L, D, D)) * D ** -0.5,
        "g_norm2": 1.0 + 0.02 * nrm(ks[19], (L, D)),
        "w_up": nrm(ks[20], (L, D, 2 * D_FF)) * D ** -0.5,
        "conv_w": nrm(ks[21], (L, CONV_W, 2 * D_FF)) * CONV_W ** -0.5,
        "conv_b": nrm(ks[22], (L, 2 * D_FF)) * 0.02,
        "w_down": nrm(ks[23], (L, D_FF, D)) * D_FF ** -0.5,
        "g_final": 1.0 + 0.02 * nrm(ks[24], (D,)),
    }


def reference(x, c, ctx, c_ctx, w_ada, b_ada, g_norm1, w_in, lam, g_diff, g_cq, w_uq, g_ckv, w_ukv, sink,
              w_branch, w_gate, b_gate, w_o, g_norm2, w_up, conv_w, conv_b, w_down, g_final):
    b = x.shape[0]
    n_rows = x.shape[1] // GRID_W
    cos64, sin64 = axial_rope_table(n_rows, DA)
    cos_b, sin_b = axial_rope_table(n_rows, ROPE_B)
    h, hc = x, ctx
    for l in range(DEPTH):
        ctx_out = l < DEPTH - 1
        lam_init = 0.8 - 0.6 * math.exp(-0.3 * l)
        mod = (jax.nn.silu(c) @ w_ada[l] + b_ada[l]).reshape(b, 1, N_MOD, D_MODEL)
        modc = (jax.nn.silu(c_ctx) @ w_ada[l] + b_ada[l]).reshape(N_MOD, D_MODEL)
        u = modulate(rms_norm(h, g_norm1[l]), mod[:, :, 0], mod[:, :, 1])
        uc = modulate(rms_norm(hc, g_norm1[l]), modc[0], modc[1])
        pl = jnp.split(u @ w_in[l], IN_SPLITS, axis=-1)
        pc = jnp.split(uc @ w_in[l], IN_SPLITS, axis=-1)
        ya, ya_c = diff_attention(pl[0:3], pc[0:3], cos64, sin64, lam[l], g_diff[l], lam_init, ctx_out)
        yb, yb_c = mla(pl[3:6], pc[3:6], cos_b, sin_b, g_cq[l], w_uq[l], g_ckv[l], w_ukv[l], ctx_out)
        yw, yw_c = window_gqa(pl[6:9], pc[6:9], cos64, sin64, sink[l], ctx_out)
        h = h + mod[:, :, 2] * merge_branches(u, (ya, yb, yw), w_branch[l], w_gate[l], b_gate[l], w_o[l])
        u2 = modulate(rms_norm(h, g_norm2[l]), mod[:, :, 3], mod[:, :, 4])
        h = h + mod[:, :, 5] * conv_ffn(u2, w_up[l], conv_w[l], conv_b[l], w_down[l])
        if ctx_out:
            hc = hc + modc[2] * merge_branches(uc, (ya_c, yb_c, yw_c), w_branch[l], w_gate[l], b_gate[l], w_o[l])
            uc2 = modulate(rms_norm(hc, g_norm2[l]), modc[3], modc[4])
            hc = hc + modc[5] * conv_ffn(uc2, w_up[l], conv_w[l], conv_b[l], w_down[l])
    return rms_norm(h, g_final)
```

```python
import math
import contextlib
import numpy as np
import concourse.bass as bass
import concourse.mybir as mybir
from concourse.bass_utils import run_bass_kernel_spmd

F32 = mybir.dt.float32
BF16 = mybir.dt.bfloat16
AF = mybir.ActivationFunctionType
ALU = mybir.AluOpType

ENGS = ("pe", "act", "dve", "pool", "sp")


class Res:
    __slots__ = ("name", "writer", "readers", "chan", "excl")

    def __init__(self, name, excl=False):
        self.name = name
        self.writer = None
        self.readers = []
        self.chan = None
        self.excl = excl


class Tok:
    __slots__ = ("key", "val", "clock")

    def __init__(self, key, val, clock):
        self.key = key
        self.val = val
        self.clock = clock


class Prog:
    def __init__(self, nc, same_eng_sync=True):
        self.nc = nc
        self.streams = {e: [] for e in ENGS}
        self.cnt = {e: 0 for e in ENGS}
        self.seen = {e: {} for e in ENGS}
        self.last = {e: None for e in ENGS}
        self.chan_cnt = {}
        self.chan_last = {}
        self.same_eng_sync = same_eng_sync
        self.all_res = []
        self.n_waits = 0
        self.n_ops = 0

    def res(self, name, excl=False):
        r = Res(name, excl)
        self.all_res.append(r)
        return r

    def _need(self, eng, tok, need):
        if tok is None:
            return
        if (not self.same_eng_sync or eng == "pe") and tok.key == eng:
            return
        if self.seen[eng].get(tok.key, 0) >= tok.val:
            return
        cur = need.get(tok.key)
        if cur is None or cur.val < tok.val:
            need[tok.key] = tok

    def _apply(self, eng, need):
        waits = []
        seen = self.seen[eng]
        for key, tok in need.items():
            waits.append((key, tok.val))
            if seen.get(key, 0) < tok.val:
                seen[key] = tok.val
            for k2, v2 in tok.clock.items():
                if seen.get(k2, 0) < v2:
                    seen[k2] = v2
        self.n_waits += len(waits)
        return waits

    def op(self, eng, fn, reads=(), writes=(), dma=None):
        need = {}
        for r in reads:
            self._need(eng, r.writer, need)
            if r.excl:
                for t in r.readers:
                    if t.key != eng:
                        self._need(eng, t, need)
        for w in writes:
            self._need(eng, w.writer, need)
            for t in w.readers:
                self._need(eng, t, need)
        if dma is not None:
            if dma.chan is None:
                dma.chan = "c%d" % len(self.chan_cnt)
                self.chan_cnt[dma.chan] = 0
                self.chan_last[dma.chan] = None
            self._need(eng, self.chan_last[dma.chan], need)
        waits = self._apply(eng, need)
        self.n_ops += 1
        clock = dict(self.seen[eng])
        if dma is not None:
            ck = dma.chan
            self.chan_cnt[ck] += 16
            tok = Tok(ck, self.chan_cnt[ck], clock)
            self.chan_last[ck] = tok
            inc = (ck, 16)
        else:
            self.cnt[eng] += 1
            tok = Tok(eng, self.cnt[eng], clock)
            self.last[eng] = tok
            inc = (eng, 1)
        self.streams[eng].append((fn, waits, inc))
        for r in reads:
            r.readers.append(tok)
        for w in writes:
            w.writer = tok
            w.readers = []
        return tok

    def barrier(self):
        toks = [t for t in self.last.values() if t is not None]
        toks += [t for t in self.chan_last.values() if t is not None]
        for eng in ENGS:
            need = {}
            for t in toks:
                if t.key == eng and eng == "pe":
                    continue
                if self.seen[eng].get(t.key, 0) >= t.val:
                    continue
                need[t.key] = t
            waits = self._apply(eng, need)
            if waits:
                self.streams[eng].append((None, waits, None))
        for r in self.all_res:
            r.writer = None
            r.readers = []

    def emit(self):
        nc = self.nc
        keys = list(ENGS) + list(self.chan_cnt.keys())
        with contextlib.ExitStack() as st:
            sems = {}
            for k in keys:
                sems[k] = st.enter_context(nc.semaphore("s_" + k))
            block = st.enter_context(nc.Block())

            def run(stream):
                def body(e):
                    for fn, waits, inc in stream:
                        for key, val in waits:
                            e.wait_ge(sems[key], val)
                        if fn is not None:
                            fn(e).then_inc(sems[inc[0]], inc[1])
                return body

            block.tensor(run(self.streams["pe"]))
            block.scalar(run(self.streams["act"]))
            block.vector(run(self.streams["dve"]))
            block.gpsimd(run(self.streams["pool"]))
            block.sync(run(self.streams["sp"]))


D = 1024
S = 2048
NCTX = 256
T = S + NCTX
NB = 2
L = 2
TILES = [(0, 256), (256, 768), (768, 1280), (1280, 1792), (1792, 2304)]
NTC = T // 128
EPS = 1e-6
DFF = 2816
NFC = DFF // 128
MLA_SCALE = 96.0 ** -0.5

QA, QAS, KA_, KAS, VA_, CQ, CKV, KR, KRS, QC, QCS, KC, KCS, VC = (
    0, 512, 1024, 1536, 2048, 2560, 2816, 2944, 3072, 3200, 3712, 4224, 4480, 4736)
NEXT = 4864


def _rope_tables():
    def tab(rot):
        nf = rot // 4
        inv = (10000.0 ** (-np.arange(nf, dtype=np.float32) / nf)).astype(np.float32)
        rows = np.repeat(np.arange(S // 64, dtype=np.float32), 64)
        cols = np.tile(np.arange(64, dtype=np.float32), S // 64)
        ang = np.concatenate([rows[:, None] * inv, cols[:, None] * inv], axis=-1)
        return np.cos(ang).astype(np.float32), np.sin(ang).astype(np.float32)
    c64, s64 = tab(64)
    c32, s32 = tab(32)
    cos64 = np.zeros((128, S), np.float32)
    sin64 = np.zeros((128, S), np.float32)
    for p in range(128):
        cos64[p] = c64[:, p % 32]
        sin64[p] = s64[:, p % 32] * (-1.0 if (p % 64) < 32 else 1.0)
    tb = np.zeros((128, S), np.float32)
    for r in range(32):
        tb[r] = c32[:, r % 16]
        tb[32 + r] = s32[:, r % 16] * (-1.0 if r < 16 else 1.0)
    return cos64, sin64, tb


def _masks():
    j = np.arange(128)[:, None]
    i = np.arange(128)[None, :]
    prev = np.where(j >= i, 0.0, -30000.0)
    nxt = np.where(j <= i, 0.0, -30000.0)
    zero = np.zeros((128, 128))
    return np.concatenate([nxt, zero, prev], axis=1).astype(np.float32)


def build(dbg=None, nb=NB, nl=L):
    nc = bass.Bass("TRN2", target_bir_lowering=False)
    dbg = dbg or ()

    def din(name, shape):
        return nc.dram_tensor(name, list(shape), F32, kind="ExternalInput").ap()

    x_d = din("x", (NB, S, D))
    ctx_d = din("ctx", (NB, NCTX, D))
    cT_d = din("cT", (128, 8, 3))
    w_ada_d = din("w_ada", (L, D, 6 * D))
    badaT_d = din("badaT", (128, L, 48))
    gn_d = din("gn", (128, L, 2, 8))
    gfin_d = din("gfinT", (128, 8))
    w_in_d = din("w_in_ext", (L, D, NEXT))
    w_uq_d = din("w_uq_ext", (L, 256, 2048))
    w_ukv_d = din("w_ukv", (L, 128, 1024))
    w_br_d = din("w_branch", (L, 3, 512, D))
    w_g_d = din("w_gate", (L, D, 3 * D))
    bgT_d = din("bgT", (128, L, 24))
    w_o_d = din("w_o", (L, D, D))
    w_up_d = din("w_up", (L, D, 2 * DFF))
    cwT_d = din("cwT", (128, L, 3, 44))
    cbT_d = din("cbT", (128, L, 44))
    w_dn_d = din("w_down", (L, DFF, D))
    small_d = din("small", (128, L, 16))
    lam_d = din("lam_rep", (128, L, 256))
    cos_d = din("cos64", (128, S))
    sin_d = din("sin64", (128, S))
    tb_d = din("tabB", (128, S))
    mask_d = din("masks", (128, 384))
    ident_d = din("ident", (128, 128))
    out_d = nc.dram_tensor("out", [NB, S, D], F32, kind="ExternalOutput").ap()
    hd = nc.dram_tensor("hd", [NB, D, T], F32).ap()
    dbg_out = {}
    for name, shape in dbg:
        dbg_out[name] = nc.dram_tensor("dbg_" + name, list(shape), F32, kind="ExternalOutput").ap()

    P = Prog(nc)
    with contextlib.ExitStack() as st:
        def sb(name, shape, dt):
            return st.enter_context(nc.sbuf_tensor(name, list(shape), dt))

        UT = sb("UT", (128, 8, T), BF16)
        YY = sb("YY", (128, 3, 4, T), BF16)
        MW = sb("MW", (128, 18432), BF16)
        SQB = sb("SQB", (128, 512), BF16)
        CQN = sb("CQN", (128, 2, 512), BF16)
        WUKV = sb("WUKV", (128, 1024), BF16)
        QT = sb("QT", (128, 2, 8, 512), BF16)
        PT = sb("PT", (128, 4, 512), BF16)
        NSTG = 4
        STG = sb("STG", (128, NSTG, 2048), BF16)
        T32 = sb("T32", (128, 8, 512), F32)
        COS = sb("COS", (128, S), BF16)
        SIN = sb("SIN", (128, S), BF16)
        TB = sb("TB", (128, S), BF16)
        MASK = sb("MASK", (128, 384), BF16)
        ONES = sb("ONES", (128, 128), BF16)
        IDB = sb("IDB", (128, 128), BF16)
        IDF = sb("IDF", (128, 128), F32)
        GFT = sb("GFT", (128, 8), F32)
        MOD = sb("MOD", (128, L, 48, 3), F32)
        AA = sb("AA", (128, L, 2, 8, 3), F32)
        BADA = sb("BADA", (128, L, 48), F32)
        GN = sb("GN", (128, L, 2, 8), F32)
        BG = sb("BG", (128, L, 24), F32)
        CW = sb("CW", (128, L, 3, 44), F32)
        CB = sb("CB", (128, L, 44), F32)
        SMALL = sb("SMALL", (128, L, 16), F32)
        LAMT = sb("LAMT", (128, L, 8), F32)
        ESINK = sb("ESINK", (128, L, 8), F32)
        GD2 = sb("GD2", (128, L, 1), F32)
        CTS = sb("CTS", (128, 8, 3), F32)
        SIL = sb("SIL", (128, 8, 3), BF16)
        EPS1K = sb("EPS1K", (128, 4), F32)
        PS = [st.enter_context(nc.psum_tensor("ps%d" % i, [128, 512], F32)) for i in range(8)]
        RPS = [P.res("ps%d" % i, excl=True) for i in range(8)]

        R = {}

        def rs(name):
            if name not in R:
                R[name] = P.res(name)
            return R[name]

        def mm(out, lhsT, rhs, start, stop, reads, writes):
            P.op("pe", lambda e: e.matmul(out, lhsT=lhsT, rhs=rhs, start=start, stop=stop),
                 reads=reads, writes=writes)

        def act(out, in_, func, reads, writes, bias=None, scale=None):
            kw = {}
            if bias is not None:
                kw["bias"] = bias
            if scale is not None:
                kw["scale"] = scale
            P.op("act", lambda e: e.activation(out=out, in_=in_, func=func, **kw), reads=reads, writes=writes)

        def cp(eng, out, in_, reads, writes):
            if eng == "act":
                P.op("act", lambda e: e.copy(out=out, in_=in_), reads=reads, writes=writes)
            else:
                P.op(eng, lambda e: e.tensor_copy(out=out, in_=in_), reads=reads, writes=writes)

        def tt(eng, out, in0, in1, op, reads, writes):
            P.op(eng, lambda e: e.tensor_tensor(out=out, in0=in0, in1=in1, op=op), reads=reads, writes=writes)

        def ts(eng, out, in0, s1, s2, op0, op1, reads, writes):
            if s2 is None:
                P.op(eng, lambda e: e.tensor_scalar(out=out, in0=in0, scalar1=s1, scalar2=None, op0=op0),
                     reads=reads, writes=writes)
            else:
                P.op(eng, lambda e: e.tensor_scalar(out=out, in0=in0, scalar1=s1, scalar2=s2, op0=op0, op1=op1),
                     reads=reads, writes=writes)

        def stt(eng, out, in0, scalar, in1, op0, op1, reads, writes):
            P.op(eng, lambda e: e.scalar_tensor_tensor(out=out, in0=in0, scalar=scalar, in1=in1, op0=op0, op1=op1),
                 reads=reads, writes=writes)

        def dma(q, out, in_, chan, reads, writes):
            return P.op(q, lambda e: e.dma_start(out=out, in_=in_), reads=reads, writes=writes, dma=chan)

        stg_i = [0]
        RSTG = [P.res("stg%d" % i) for i in range(NSTG)]

        def stage(src, kc, ncol):
            i = stg_i[0] % NSTG
            stg_i[0] += 1
            view = STG[:, i, 0:kc * ncol].rearrange("p (k n) -> p k n", k=kc)
            dma("pool", view, src, RSTG[i], [], [RSTG[i]])
            return view, RSTG[i]

        def wview(w2d, c0, ncol):
            return w2d[:, c0:c0 + ncol].rearrange("(k p) n -> p k n", p=128)

        def rstd_from(ps_ap, out_ap, n_feat, reads, writes):
            col = {D: 0, 128: 1, 256: 2}[n_feat]
            act(out_ap, ps_ap, AF.Ln, list(reads) + [RC], writes, bias=EPS1K[:, col:col + 1])
            act(out_ap, out_ap, AF.Exp, writes, writes, scale=-0.5)

        RC = rs("consts")
        for ci, (dst, src) in enumerate(((COS, cos_d), (SIN, sin_d), (TB, tb_d), (IDB, ident_d))):
            dma("pool", dst[:], src[:, :], rs("c_%d" % ci), [], [RC])
        dma("pool", MASK[:], mask_d[:, :], rs("c_mask"), [], [RC])
        LAMR = T32[:, 1, :].rearrange("p (l n) -> p l n", l=L)
        for dst, src in ((IDF, ident_d), (GFT, gfin_d)):
            dma("sp", dst[:], src[:, :], rs("c2_" + str(id(dst))), [], [RC])
        dma("sp", LAMR, lam_d[:, :, :], rs("c3_lam"), [], [RC])
        for dst, src in ((BADA, badaT_d), (BG, bgT_d), (CB, cbT_d), (SMALL, small_d), (CTS, cT_d)):
            dma("sp", dst[:], src[:, :, :], rs("c3_" + str(id(dst))), [], [RC])
        for dst, src in ((GN, gn_d), (CW, cwT_d)):
            dma("sp", dst[:], src[:, :, :, :], rs("c4_" + str(id(dst))), [], [RC])
        P.op("dve", lambda e: e.memset(ONES[:], 1.0), writes=[RC])
        P.op("dve", lambda e: e.memset(EPS1K[:, 0:1], 0.001024), writes=[RC])
        P.op("dve", lambda e: e.memset(EPS1K[:, 1:2], 0.000128), writes=[RC])
        P.op("dve", lambda e: e.memset(EPS1K[:, 2:3], 0.000256), writes=[RC])
        P.op("dve", lambda e: e.memset(EPS1K[:, 3:4], 0.0), writes=[RC])
        P.barrier()

        lam_init = [0.8 - 0.6 * math.exp(-0.3 * l) for l in range(L)]
        for l in range(L):
            tmp = T32[:, 0, 0:128]
            tt("dve", tmp[:, 0:64], LAMR[:, l, 0:64], LAMR[:, l, 64:128], ALU.mult, [RC], [rs("t0")])
            tt("dve", tmp[:, 64:128], LAMR[:, l, 128:192], LAMR[:, l, 192:256], ALU.mult, [RC], [rs("t0")])
            P.op("dve", lambda e, l=l: e.reduce_sum(out=LAMT[:, l, 0:1], in_=T32[:, 0, 0:64], axis=mybir.AxisListType.X),
                 reads=[rs("t0")], writes=[rs("lamt")])
            P.op("dve", lambda e, l=l: e.reduce_sum(out=LAMT[:, l, 1:2], in_=T32[:, 0, 64:128], axis=mybir.AxisListType.X),
                 reads=[rs("t0")], writes=[rs("lamt")])
            act(LAMT[:, l, 2:4], LAMT[:, l, 0:2], AF.Exp, [rs("lamt")], [rs("lamt")])
            tt("dve", LAMT[:, l, 4:5], LAMT[:, l, 3:4], LAMT[:, l, 2:3], ALU.subtract, [rs("lamt")], [rs("lamt")])
            ts("dve", LAMT[:, l, 7:8], LAMT[:, l, 4:5], -lam_init[l], None, ALU.add, None, [rs("lamt")], [rs("lamt")])
            act(ESINK[:, l, :], SMALL[:, l, 4:12], AF.Exp, [RC], [rs("lamt")])
            ts("dve", GD2[:, l, :], SMALL[:, l, 0:1], float((1.0 - lam_init[l]) * math.sqrt(128.0)), None, ALU.mult, None,
               [RC], [rs("lamt")])
        act(SIL[:], CTS[:], AF.Silu, [RC], [rs("sil")])
        MODR = sb("MODR", (128, L, 144), F32)

        def mod_piece(l, pc):
            wv, wr = stage(wview(w_ada_d[l], pc * 256, 256), 8, 256)
            for jj in range(2):
                for k in range(8):
                    mm(PS[6][:, jj * 3:(jj + 1) * 3], wv[:, k, jj * 128:(jj + 1) * 128], SIL[:, k, :], k == 0, k == 7,
                       [wr, rs("sil")], [RPS[6]])
            cp("dve", MODR[:, l, pc * 6:(pc + 1) * 6], PS[6][:, 0:6], [RPS[6]], [rs("modr")])

        def mod_finish(l):
            m3 = MODR[:, l, :].rearrange("p (j w) -> p j w", w=3)
            for w in range(3):
                tt("dve", MOD[:, l, :, w], m3[:, :, w], BADA[:, l, :], ALU.add, [rs("modr"), RC], [rs("mod")])
            for w in range(3):
                for which, j0 in ((0, 8), (1, 32)):
                    stt("dve", AA[:, l, which, :, w], MOD[:, l, j0:j0 + 8, w], 1.0, GN[:, l, which, :], ALU.add, ALU.mult,
                        [rs("mod"), RC], [rs("aa")])
            P.op("dve", lambda e: e.tensor_scalar(out=AAS[:, l], in0=AA[:, l], scalar1=SQRT_D, scalar2=None, op0=ALU.mult),
                 reads=[rs("aa")], writes=[rs("aa")])

        SQRT_D = math.sqrt(float(D))
        AAS = sb("AAS", (128, L, 2, 8, 3), F32)
        RSTD = sb("RSTD", (128, 512), F32)
        RSTDF = PT[:].rearrange("p a n -> p (a n)").bitcast(F32).rearrange("p (s n) -> p s n", s=2)
        GCKV2 = sb("GCKV2", (128, L, 1), F32)
        GCQ2 = sb("GCQ2", (128, L, 2), F32)
        GFS = sb("GFS", (128, 8), F32)
        P.op("dve", lambda e: e.tensor_scalar(out=GCKV2[:], in0=SMALL[:, :, 3:4], scalar1=math.sqrt(128.0), scalar2=None, op0=ALU.mult),
             reads=[RC], writes=[rs("lamt")])
        P.op("dve", lambda e: e.tensor_scalar(out=GCQ2[:], in0=SMALL[:, :, 1:3], scalar1=math.sqrt(256.0), scalar2=None, op0=ALU.mult),
             reads=[RC], writes=[rs("lamt")])
        P.op("dve", lambda e: e.tensor_scalar(out=GFS[:], in0=GFT[:], scalar1=SQRT_D, scalar2=None, op0=ALU.mult),
             reads=[RC], writes=[rs("lamt")])
        P.barrier()

        def scal(l, kind, c, w):
            if kind == "A1":
                return AAS[:, l, 0, c, w:w + 1]
            if kind == "A2":
                return AAS[:, l, 1, c, w:w + 1]
            j0 = {"B1": 0, "G1": 16, "B2": 24, "G2": 40}[kind]
            return MOD[:, l, j0 + c, w:w + 1]

        NT = len(TILES)
        hdv = [hd[b].rearrange("(c p) t -> p c t", p=128) for b in range(NB)]
        RHD = [[[P.res("hd%d_%d_%d" % (b, i, c)) for c in range(8)] for i in range(NT)] for b in range(NB)]
        RUT = [P.res("ut%d" % i) for i in range(NT)]
        RT = [P.res("t32_%d" % i) for i in range(8)]
        RPT = [P.res("pt%d" % i) for i in range(4)]
        RQT = [P.res("qt%d" % i) for i in range(2)]
        RSQB = P.res("sqb")
        RCQN = P.res("cqn")
        RRS = P.res("rstd")

        def tile_of_chunk(tc):
            t = tc * 128
            for i, (a, b_) in enumerate(TILES):
                if a <= t < b_:
                    return i
            raise ValueError

        def phase_in(b):
            XTS = MW[:, 0:8192].bitcast(F32).rearrange("p (s n) -> p s n", s=4)
            HTB = [T32, MW[:, 8192:16384].bitcast(F32).rearrange("p (c n) -> p c n", c=8)]
            RX = [rs("pin_x%d" % i) for i in range(4)]
            RH = [rs("pin_h%d" % i) for i in range(2)]
            cnt = 0
            for ti in range(NT):
                t0, t1 = TILES[ti]
                n = t1 - t0
                hb = HTB[ti % 2]
                for tb in range(n // 128):
                    tok = t0 + tb * 128
                    src = ctx_d[b, tok:tok + 128, :] if tok < NCTX else x_d[b, tok - NCTX:tok - NCTX + 128, :]
                    slot = cnt % 4
                    cnt += 1
                    xt = XTS[:, slot, :]
                    dma("sp", xt, src, RX[slot], [], [RX[slot]])
                    for half in range(2):
                        pb = PS[slot * 2 + half]
                        for cc in range(4):
                            c = half * 4 + cc
                            P.op("pe", lambda e, pb=pb, cc=cc, c=c, xt=xt: e.transpose(
                                pb[:, cc * 128:(cc + 1) * 128], xt[:, c * 128:(c + 1) * 128], IDF[:]),
                                reads=[RX[slot], RC], writes=[RPS[slot * 2 + half]])
                    cp("dve", hb[:, 0:4, tb * 128:(tb + 1) * 128], PS[slot * 2][:].rearrange("p (c n) -> p c n", c=4),
                       [RPS[slot * 2]], [RH[ti % 2]])
                    cp("act", hb[:, 4:8, tb * 128:(tb + 1) * 128], PS[slot * 2 + 1][:].rearrange("p (c n) -> p c n", c=4),
                       [RPS[slot * 2 + 1]], [RH[ti % 2]])
                dma("sp", hdv[b][:, :, t0:t1], hb[:, :, 0:n], RH[ti % 2], [RH[ti % 2]], RHD[b][ti])
            P.barrier()

        def phase_norm(b, l, which, tiles):
            kA, kB = ("A1", "B1") if which == 1 else ("A2", "B2")
            HTS = MW[:, 0:16384].bitcast(F32).rearrange("p (s c n) -> p s c n", s=2, c=8)
            SQS = STG[:].rearrange("p a n -> p (a n)").rearrange("p (s c n) -> p s c n", s=2, c=8)
            RSQ = [rs("nsq0"), rs("nsq1")]
            RHC = [[rs("nht%d_%d" % (s_, c)) for c in range(8)] for s_ in range(2)]
            RRSN = [rs("nrs0"), rs("nrs1")]
            tiles = list(tiles)

            def head(i):
                ti = tiles[i]
                t0, t1 = TILES[ti]
                n = t1 - t0
                s_ = i % 2
                HT = HTS[:, s_]
                SQ = SQS[:, s_]
                dma("sp", HT[:, :, 0:n], hdv[b][:, :, t0:t1], RHC[s_][0], RHD[b][ti], RHC[s_])
                tt("dve", SQ[:, 0:4, 0:n], HT[:, 0:4, 0:n], HT[:, 0:4, 0:n], ALU.mult, RHC[s_][0:4], [RSQ[s_]])
                tt("pool", SQ[:, 4:8, 0:n], HT[:, 4:8, 0:n], HT[:, 4:8, 0:n], ALU.mult, RHC[s_][4:8], [RSQ[s_]])

            def tail(i):
                ti = tiles[i]
                t0, t1 = TILES[ti]
                n = t1 - t0
                w = 2 if ti == 0 else b
                s_ = i % 2
                HT = HTS[:, s_]
                SQ = SQS[:, s_]
                RS_ = T32[:, s_, 0:n]
                pb = s_
                for c in range(8):
                    mm(PS[pb][:, 0:n], ONES[:], SQ[:, c, 0:n], c == 0, c == 7, [RSQ[s_], RC], [RPS[pb]])
                rstd_from(PS[pb][:, 0:n], RS_, D, [RPS[pb]], [RRSN[s_]])
                for c in range(8):
                    tt("dve", HT[:, c, 0:n], HT[:, c, 0:n], RS_, ALU.mult, [RHC[s_][c], RRSN[s_]], [RHC[s_][c]])
                    act(UT[:, c, t0:t1], HT[:, c, 0:n], AF.Identity, [RHC[s_][c], rs("aa"), rs("mod")], [RUT[ti]],
                        bias=scal(l, kB, c, w), scale=scal(l, kA, c, w))

            head(0)
            for i in range(len(tiles)):
                if i + 1 < len(tiles):
                    head(i + 1)
                tail(i)
            P.barrier()

        def run_attn(jobs, nq, LA=3, hooks=None, nsb=4):
            seq = [(j, c) for j, job in enumerate(jobs) for c in range(len(job["chunks"]))]
            hooks = list(hooks or [])
            for idx in range(len(seq) + LA):
                if idx < len(seq):
                    j, c = seq[idx]
                    job = jobs[j]
                    if c == 0 and hooks:
                        hooks.pop(0)()
                    ch = job["chunks"][c]
                    kT, v, mask = ch[0], ch[1], ch[2]
                    c0, c1 = (ch[3], ch[4]) if len(ch) > 3 else (0, nq)
                    sbk = idx % nsb
                    mm(PS[sbk][:, c0:c1], kT, job["q"][:, c0:c1], True, mask is None, job["reads"], [RPS[sbk]])
                    if mask is not None:
                        mm(PS[sbk][:, c0:c1], IDB[:], mask, False, True, [RC], [RPS[sbk]])
                    pslot = idx % 4
                    act(PT[:, pslot, c0:c1], PS[sbk][:, c0:c1], AF.Exp, [RPS[sbk]], [RPT[pslot]], scale=job["scale"])
                if idx - LA >= 0:
                    j, c = seq[idx - LA]
                    job = jobs[j]
                    ch = job["chunks"][c]
                    kT, v, mask = ch[0], ch[1], ch[2]
                    c0, c1 = (ch[3], ch[4]) if len(ch) > 3 else (0, nq)
                    nchunk = len(job["chunks"])
                    pslot = (idx - LA) % 4
                    acc_ap, acc_res = job["acc"]
                    mm(acc_ap[:, c0:c1], v, PT[:, pslot, c0:c1], c == 0, c == nchunk - 1, job["reads"] + [RPT[pslot]], [acc_res])
                    if job.get("den") is not None:
                        den_ap, den_res, d32, d32r, dbf, dbfr = job["den"]
                        if c % 2 == 0:
                            if c == 0:
                                cp("dve", d32, PT[:, pslot, 0:nq], [RPT[pslot]], [d32r])
                            else:
                                tt("dve", d32, d32, PT[:, pslot, 0:nq], ALU.add, [d32r, RPT[pslot]], [d32r])
                        else:
                            mm(den_ap, ONES[:], PT[:, pslot, 0:nq], c == 1, False, [RPT[pslot], RC], [den_res])
                        if c == nchunk - 1:
                            cp("dve", dbf, d32, [d32r], [dbfr])
                            mm(den_ap, ONES[:], dbf, False, True, [dbfr, RC], [den_res])
                    if c == nchunk - 1:
                        job["fin"]()
            for h_ in hooks:
                h_()

        def rope_evac(outs, ps_a, ps_b, cos_ap, sin_ap, rd_a, rd_b, wr, tmpi, n):
            t1 = T32[:, tmpi, 0:n]
            t2 = T32[:, tmpi + 1, 0:n]
            tt("dve", t1, ps_a, cos_ap, ALU.mult, [rd_a, RC], [RT[tmpi]])
            tt("dve", t2, ps_b, sin_ap, ALU.mult, [rd_b, RC], [RT[tmpi + 1]])
            for rows, out_ap in outs:
                tt("pool", out_ap, T32[rows, tmpi, 0:n], T32[rows, tmpi + 1, 0:n], ALU.add, [RT[tmpi], RT[tmpi + 1]], wr)

        ALLROWS = slice(0, 128)

        def proj_rope(l, wcol, wcol_s, nm, dst_fn, dst_res_fn, tiles, bank0=4, npair=2, steps=None):
            cnt = [0]

            def one(mc, wa, ra, wb_, rb, mi):
                for ti in tiles:
                    t0, t1 = TILES[ti]
                    n = t1 - t0
                    ba = bank0 + (cnt[0] % npair) * 2
                    tm = (cnt[0] % 2) * 2
                    cnt[0] += 1
                    for k in range(8):
                        mm(PS[ba][:, 0:n], wa[:, k, mi * 128:(mi + 1) * 128], UT[:, k, t0:t1], k == 0, k == 7,
                           [ra, RUT[ti]], [RPS[ba]])
                    outs = dst_fn(mc, t0, t1)
                    if ti == 0:
                        for rows, out_ap in outs:
                            cp("dve", out_ap, PS[ba][rows, 0:n], [RPS[ba]], dst_res_fn(mc, ti))
                    else:
                        for k in range(8):
                            mm(PS[ba + 1][:, 0:n], wb_[:, k, mi * 128:(mi + 1) * 128], UT[:, k, t0:t1], k == 0, k == 7,
                               [rb, RUT[ti]], [RPS[ba + 1]])
                        rope_evac(outs, PS[ba][:, 0:n], PS[ba + 1][:, 0:n],
                                  COS[:, t0 - NCTX:t1 - NCTX], SIN[:, t0 - NCTX:t1 - NCTX],
                                  RPS[ba], RPS[ba + 1], dst_res_fn(mc, ti), tm, n)

            for mp in range(0, nm, 2):
                nmc = min(2, nm - mp)

                def grp(mp=mp, nmc=nmc):
                    wa, ra = stage(wview(w_in_d[l], wcol + mp * 128, nmc * 128), 8, nmc * 128)
                    wb_, rb = stage(wview(w_in_d[l], wcol_s + mp * 128, nmc * 128), 8, nmc * 128)
                    return wa, ra, wb_, rb
                if steps is None:
                    wa, ra, wb_, rb = grp()
                    for mi in range(nmc):
                        one(mp + mi, wa, ra, wb_, rb, mi)
                else:
                    box = {}

                    def st0(mp=mp, box=box, grp=grp):
                        box["w"] = grp()
                        one(mp, *box["w"], 0)
                    steps.append(st0)
                    if nmc > 1:
                        steps.append(lambda mp=mp, box=box: one(mp + 1, *box["w"], 1))

        def zero_qt():
            P.op("dve", lambda e: e.memset(QT[:, 0], 0.0), writes=[RQT[0]])
            P.op("pool", lambda e: e.memset(QT[:, 1], 0.0), writes=[RQT[1]])

        LO = slice(0, 64)
        HI = slice(64, 128)

        def mixer_A(b, l, qtiles, extra=None):
            KAt = MW[:, 0:4 * T].rearrange("p (c t) -> p c t", c=4)
            VAt = MW[:, 4 * T:4 * T + NTC * 512].rearrange("p (c n) -> p c n", c=NTC)
            RKA = [P.res("ka%d" % i) for i in range(NT)]
            RVA = [P.res("va%d" % i) for i in range(NT)]
            zero_qt()
            proj_rope(l, KA_, KAS, 4, lambda mc, t0, t1: [(ALLROWS, KAt[:, mc, t0:t1])], lambda mc, ti: [RKA[ti]], range(NT))
            wv0, rv0 = stage(wview(w_in_d[l], VA_, 256), 8, 256)
            wv1, rv1 = stage(wview(w_in_d[l], VA_ + 256, 256), 8, 256)
            for tc in range(NTC):
                ti = tile_of_chunk(tc)
                bk = 4 + tc % 2
                for hf, (wv, rv) in enumerate(((wv0, rv0), (wv1, rv1))):
                    for k in range(8):
                        mm(PS[bk][:, hf * 256:(hf + 1) * 256], UT[:, k, tc * 128:(tc + 1) * 128], wv[:, k, :], k == 0, k == 7,
                           [rv, RUT[ti]], [RPS[bk]])
                cp("act" if tc % 2 else "dve", VAt[:, tc, :], PS[bk][:], [RPS[bk]], [RVA[ti]])
            kall = RKA + RVA
            DEN32 = YY[:, 2, 2:4, :].rearrange("p a n -> p (a n)")[:, 0:4096].bitcast(F32).rearrange("p (a n) -> p a n", a=4)
            DENB = CQN
            RD32 = [P.res("d32_%d" % i) for i in range(4)]
            RDB = [P.res("dbf_%d" % i) for i in range(2)]

            def qsteps(qi, ti):
                t0, t1 = TILES[ti]
                n = t1 - t0
                qs = qi % 2
                st_ = []
                proj_rope(l, QA, QAS, 4, lambda mc, a, b_, qs=qs, n=n: [(LO, QT[LO, qs, mc * 2, 0:n]), (HI, QT[HI, qs, mc * 2 + 1, 0:n])],
                          lambda mc, ti_, qs=qs: [RQT[qs]], [ti], bank0=6, npair=1, steps=st_)
                return st_

            for f_ in qsteps(0, qtiles[0]):
                f_()
            for qi, ti in enumerate(qtiles):
                t0, t1 = TILES[ti]
                n = t1 - t0
                qs = qi % 2
                hooks = qsteps(qi + 1, qtiles[qi + 1]) if qi + 1 < len(qtiles) else []
                hk = []
                hooks = hooks + [(lambda: None)] * (4 - len(hooks))
                for f_ in hooks:
                    hk += [f_, (extra.pop(0) if extra else (lambda: None))]
                kchunks = range(2) if ti == 0 else range(NTC)
                jobs = []
                for h in range(4):
                    for i in range(2):
                        jn = h * 2 + i
                        bo = 3 + jn % 2
                        jp = jn % 2

                        def fin(h=h, i=i, bo=bo, n=n, t0=t0, t1=t1, ti=ti):
                            rd = T32[:, 4, 0:n]
                            cp("dve", rd, PS[5][:, 0:n], [RPS[5]], [RT[4]])
                            act(rd, rd, AF.Ln, [RT[4]], [RT[4]])
                            act(rd, rd, AF.Exp, [RT[4]], [RT[4]], scale=-1.0)
                            on = T32[:, 5 + i, 0:n]
                            tt("dve", on, PS[bo][:, 0:n], rd, ALU.mult, [RPS[bo], RT[4]], [RT[5 + i]])
                            if i == 1:
                                o = T32[:, 7, 0:n]
                                stt("dve", o, T32[:, 6, 0:n], LAMT[:, l, 7:8], T32[:, 5, 0:n], ALU.mult, ALU.add,
                                    [RT[5], RT[6], rs("lamt")], [RT[7]])
                                sq = SQB[:, 0:n]
                                tt("pool", sq, o, o, ALU.mult, [RT[7]], [RSQB])
                                mm(PS[7][:, 0:n], ONES[:], sq, True, True, [RSQB, RC], [RPS[7]])
                                rstd_from(PS[7][:, 0:n], RSTD[:, 0:n], 128, [RPS[7]], [RRS])
                                stt("dve", YY[:, 0, h, t0:t1], o, GD2[:, l, 0:1], RSTD[:, 0:n], ALU.mult, ALU.mult,
                                    [RT[7], RRS, rs("lamt")], [rs("ya%d" % ti)])
                        jobs.append(dict(
                            q=QT[:, qs, h * 2 + i, 0:n],
                            chunks=[(KAt[:, h, kc * 128:(kc + 1) * 128], VAt[:, kc, h * 128:(h + 1) * 128], None) for kc in kchunks],
                            scale=0.125, acc=(PS[bo][:, 0:n], RPS[bo]),
                            den=(PS[5][:, 0:n], RPS[5], DEN32[:, jp, 0:n], RD32[jp], DENB[:, jp, 0:n], RDB[jp]),
                            reads=kall + [RQT[qs]], fin=fin))
                run_attn(jobs, n, LA=2, hooks=hk, nsb=3)
            P.barrier()

        def mixer_B(b, l, qtiles):
            CKVN = YY[:, 2, 0, :]
            KRt = YY[:, 2, 1, :]
            RCK = rs("ckvn")
            RKR = rs("krt")
            RWU = rs("wukv")
            dma("pool", WUKV[:], w_ukv_d[l], RWU, [], [RWU])
            wck, rck = stage(wview(w_in_d[l], CKV, 128), 8, 128)
            wkr, rkr = stage(wview(w_in_d[l], KR, 256), 8, 256)
            for ti in range(NT):
                t0, t1 = TILES[ti]
                n = t1 - t0
                for k in range(8):
                    mm(PS[4][:, 0:n], wck[:, k, :], UT[:, k, t0:t1], k == 0, k == 7, [rck, RUT[ti]], [RPS[4]])
                c1 = T32[:, 0, 0:n]
                cp("act", c1, PS[4][:, 0:n], [RPS[4]], [RT[0]])
                sq = SQB[:, 0:n]
                tt("pool", sq, c1, c1, ALU.mult, [RT[0]], [RSQB])
                mm(PS[5][:, 0:n], ONES[:], sq, True, True, [RSQB, RC], [RPS[5]])
                rstd_from(PS[5][:, 0:n], RSTD[:, 0:n], 128, [RPS[5]], [RRS])
                stt("dve", CKVN[:, t0:t1], c1, GCKV2[:, l, 0:1], RSTD[:, 0:n], ALU.mult, ALU.mult, [RT[0], RRS, rs("lamt")], [RCK])
                for k in range(8):
                    mm(PS[6][:, 0:n], wkr[:, k, 0:128], UT[:, k, t0:t1], k == 0, k == 7, [rkr, RUT[ti]], [RPS[6]])
                if ti == 0:
                    cp("act", KRt[64:96, t0:t1], PS[6][64:96, 0:n], [RPS[6]], [RKR])
                else:
                    for k in range(8):
                        mm(PS[7][:, 0:n], wkr[:, k, 128:256], UT[:, k, t0:t1], k == 0, k == 7, [rkr, RUT[ti]], [RPS[7]])
                    ta = T32[64:96, 1, 0:n]
                    tb_ = T32[64:96, 2, 0:n]
                    tt("dve", ta, PS[6][64:96, 0:n], TB[0:32, t0 - NCTX:t1 - NCTX], ALU.mult, [RPS[6], RC], [RT[1]])
                    tt("dve", tb_, PS[7][64:96, 0:n], TB[32:64, t0 - NCTX:t1 - NCTX], ALU.mult, [RPS[7], RC], [RT[2]])
                    tt("pool", KRt[64:96, t0:t1], ta, tb_, ALU.add, [RT[1], RT[2]], [RKR])
            CQNA = YY[:, 2, 2:4, :]
            RCQA = rs("cqna")
            wcq, rcq = stage(wview(w_in_d[l], CQ, 256), 8, 256)
            for ti in qtiles:
                t0, t1 = TILES[ti]
                n = t1 - t0
                for mc in range(2):
                    for k in range(8):
                        mm(PS[mc][:, 0:n], wcq[:, k, mc * 128:(mc + 1) * 128], UT[:, k, t0:t1], k == 0, k == 7,
                           [rcq, RUT[ti]], [RPS[mc]])
                    cp("act", T32[:, 4 + mc, 0:n], PS[mc][:, 0:n], [RPS[mc]], [RT[4 + mc]])
                    tt("pool", CQN[:, mc, 0:n], T32[:, 4 + mc, 0:n], T32[:, 4 + mc, 0:n], ALU.mult, [RT[4 + mc]], [RCQN])
                for mc in range(2):
                    mm(PS[2][:, 0:n], ONES[:], CQN[:, mc, 0:n], mc == 0, mc == 1, [RCQN, RC], [RPS[2]])
                rstd_from(PS[2][:, 0:n], RSTD[:, 0:n], 256, [RPS[2]], [RRS])
                for mc in range(2):
                    stt("dve", CQNA[:, mc, t0:t1], T32[:, 4 + mc, 0:n], GCQ2[:, l, mc:mc + 1], RSTD[:, 0:n], ALU.mult, ALU.mult,
                        [RT[4 + mc], RRS, rs("lamt")], [RCQA])
            P.barrier()
            KBt = MW[:, 0:4 * T].rearrange("p (c t) -> p c t", c=4)
            VBt = MW[:, 4 * T:4 * T + NTC * 512].rearrange("p (c h n) -> p c h n", c=NTC, h=4)
            for hg in range(2):
                RKB = rs("kb")
                RVB = rs("vb")
                if hg == 0:
                    P.op("pool", lambda e: e.memset(VBt[:], 1.0), writes=[RVB])
                for hh in range(4):
                    h = hg * 4 + hh
                    for ti in range(NT):
                        t0, t1 = TILES[ti]
                        n = t1 - t0
                        bk = (hh * 5 + ti) % 4
                        mm(PS[bk][0:64, 0:n], WUKV[:, h * 128:h * 128 + 64], CKVN[:, t0:t1], True, True, [RWU, RCK], [RPS[bk]])
                        cp("act" if ti % 2 else "dve", KBt[0:64, hh, t0:t1], PS[bk][0:64, 0:n], [RPS[bk]], [RKB])
                    cp("dve", KBt[64:96, hh, :], KRt[64:96, :], [RKR], [RKB])
                WV4 = WUKV[:].rearrange("p (h n) -> p h n", h=8)[:, hg * 4:(hg + 1) * 4, 64:128]
                for tc in range(NTC):
                    bk = 4 + tc % 4
                    ps4 = PS[bk][:, 0:256].rearrange("p (h n) -> p h n", h=4)
                    mm(ps4, CKVN[:, tc * 128:(tc + 1) * 128], WV4, True, True, [RWU, RCK], [RPS[bk]])
                    eng = "act" if tc % 2 else "dve"
                    cp(eng, VBt[:, tc, 0:4:2, 0:64], ps4[:, 0:4:2, :], [RPS[bk]], [RVB])
                    cp(eng, VBt[:, tc, 1:4:2, 64:128], ps4[:, 1:4:2, :], [RPS[bk]], [RVB])
                wuq, ruq = stage(w_uq_d[l][:, hg * 512:(hg + 1) * 512].rearrange("(k p) n -> p k n", p=128), 2, 512)
                wus, rus = stage(w_uq_d[l][:, 1024 + hg * 512:1024 + (hg + 1) * 512].rearrange("(k p) n -> p k n", p=128), 2, 512)

                def qsteps(qi, ti, wuq=wuq, ruq=ruq, wus=wus, rus=rus):
                    t0, t1 = TILES[ti]
                    n = t1 - t0
                    qs = qi % 2
                    st_ = []
                    for hh in range(4):
                        def one(hh=hh, t0=t0, t1=t1, n=n, qs=qs, ti=ti):
                            for mc in range(2):
                                mm(PS[6][0:96, 0:n], wuq[:, mc, hh * 128:hh * 128 + 96], CQNA[:, mc, t0:t1], mc == 0, mc == 1,
                                   [ruq, RCQA], [RPS[6]])
                            if ti == 0:
                                cp("dve", QT[0:96, qs, hh, 0:n], PS[6][0:96, 0:n], [RPS[6]], [RQT[qs]])
                            else:
                                cp("dve", QT[0:64, qs, hh, 0:n], PS[6][0:64, 0:n], [RPS[6]], [RQT[qs]])
                                for mc in range(2):
                                    mm(PS[7][0:96, 0:n], wus[:, mc, hh * 128:hh * 128 + 96], CQNA[:, mc, t0:t1], mc == 0, mc == 1,
                                       [rus, RCQA], [RPS[7]])
                                ta = T32[64:96, 2, 0:n]
                                tb_ = T32[64:96, 3, 0:n]
                                tt("dve", ta, PS[6][64:96, 0:n], TB[0:32, t0 - NCTX:t1 - NCTX], ALU.mult, [RPS[6], RC], [RT[2]])
                                tt("dve", tb_, PS[7][64:96, 0:n], TB[32:64, t0 - NCTX:t1 - NCTX], ALU.mult, [RPS[7], RC], [RT[3]])
                                tt("pool", QT[64:96, qs, hh, 0:n], ta, tb_, ALU.add, [RT[2], RT[3]], [RQT[qs]])
                        st_.append(one)
                    return st_

                for f_ in qsteps(0, qtiles[0]):
                    f_()
                for qi, ti in enumerate(qtiles):
                    t0, t1 = TILES[ti]
                    n = t1 - t0
                    qs = qi % 2
                    hooks = qsteps(qi + 1, qtiles[qi + 1]) if qi + 1 < len(qtiles) else []
                    kchunks = range(2) if ti == 0 else range(NTC)
                    jobs = []
                    for hh in range(4):
                        h = hg * 4 + hh
                        bo = 4 + hh % 2
                        even = (hh % 2 == 0)
                        orow = slice(0, 64) if even else slice(64, 128)
                        drow = slice(64, 128) if even else slice(0, 64)

                        def fin(h=h, bo=bo, orow=orow, drow=drow, n=n, t0=t0, t1=t1, ti=ti):
                            rd = T32[drow, 4, 0:n]
                            P.op("dve", lambda e: e.reciprocal(out=rd, in_=PS[bo][drow, 0:n]), reads=[RPS[bo]], writes=[RT[4]])
                            tt("dve", YY[orow, 1, h // 2, t0:t1], PS[bo][orow, 0:n], rd, ALU.mult, [RPS[bo], RT[4]], [rs("yb%d" % ti)])
                        jobs.append(dict(
                            q=QT[0:96, qs, hh, 0:n],
                            chunks=[(KBt[0:96, hh, kc * 128:(kc + 1) * 128], VBt[:, kc, hh, :], None) for kc in kchunks],
                            scale=MLA_SCALE, acc=(PS[bo][:, 0:n], RPS[bo]), den=None,
                            reads=[RKB, RVB, RQT[qs]], fin=fin))
                    run_attn(jobs, n, hooks=hooks)
                P.barrier()

        def mixer_C(b, l, qtiles):
            KCt = MW[:, 0:T].rearrange("p (c t) -> p c t", c=1)
            VCt = MW[:, 2 * T:2 * T + NTC * 384].rearrange("p (c g n) -> p c g n", c=NTC, g=2)
            RKC = [P.res("kc%d" % i) for i in range(NT)]
            RVC = rs("vc")
            zero_qt()
            proj_rope(l, KC, KCS, 1, lambda mc, t0, t1: [(ALLROWS, KCt[:, mc, t0:t1])], lambda mc, ti: [RKC[ti]], range(NT))
            P.op("pool", lambda e: e.memset(VCt[:], 1.0), writes=[RVC])
            wv, rv = stage(wview(w_in_d[l], VC, 128), 8, 128)
            for tc in range(NTC):
                ti = tile_of_chunk(tc)
                bk = 4 + tc % 2
                for k in range(8):
                    mm(PS[bk][:, 0:128], UT[:, k, tc * 128:(tc + 1) * 128], wv[:, k, :], k == 0, k == 7, [rv, RUT[ti]], [RPS[bk]])
                cp("act" if tc % 2 else "dve", VCt[:, tc, :, 64:128], PS[bk][:, 0:128].rearrange("p (g n) -> p g n", g=2),
                   [RPS[bk]], [RVC])

            def qsteps(qi, ti):
                t0, t1 = TILES[ti]
                n = t1 - t0
                qs = qi % 2
                st_ = []
                proj_rope(l, QC, QCS, 4, lambda mc, a, b_, qs=qs, n=n: [(LO, QT[LO, qs, mc, 0:n]), (HI, QT[HI, qs, mc + 4, 0:n])],
                          lambda mc, ti_, qs=qs: [RQT[qs]], [ti], bank0=6, npair=1, steps=st_)
                return st_

            for f_ in qsteps(0, qtiles[0]):
                f_()
            for qi, ti in enumerate(qtiles):
                t0, t1 = TILES[ti]
                n = t1 - t0
                qs = qi % 2
                hooks = qsteps(qi + 1, qtiles[qi + 1]) if qi + 1 < len(qtiles) else []
                hk = []
                for f_ in hooks:
                    hk += [f_, (lambda: None)]
                jobs = []
                for hq in range(8):
                    g = hq // 4
                    even = (hq % 2 == 0)
                    rows = slice(0, 64) if even else slice(64, 128)
                    drow = slice(64, 128) if even else slice(0, 64)
                    vs = slice(64, 192) if even else slice(0, 128)
                    bo = 4 + hq % 2
                    chunks = [(KCt[:, 0, kc * 128:(kc + 1) * 128], VCt[:, kc, g, vs], None) for kc in range(2)]
                    if ti > 0:
                        n0 = (ti - 1) * 4
                        for d in range(6):
                            m = n0 + d - 1
                            if m < 0 or m >= 16:
                                continue
                            kc = m + 2
                            b0 = max(0, d - 2)
                            b1 = min(3, d)
                            c0, c1 = b0 * 128, (b1 + 1) * 128
                            p0 = (2 - d + b0) * 128
                            chunks.append((KCt[:, 0, kc * 128:(kc + 1) * 128], VCt[:, kc, g, vs], MASK[:, p0:p0 + (c1 - c0)], c0, c1))

                    def fin(hq=hq, bo=bo, rows=rows, drow=drow, n=n, t0=t0, t1=t1, ti=ti):
                        rd = T32[drow, 4, 0:n]
                        act(rd, PS[bo][drow, 0:n], AF.Ln, [RPS[bo], rs("lamt")], [RT[4]], bias=ESINK[drow, l, hq:hq + 1])
                        act(rd, rd, AF.Exp, [RT[4]], [RT[4]], scale=-1.0)
                        tt("dve", YY[rows, 2, hq // 2, t0:t1], PS[bo][rows, 0:n], rd, ALU.mult, [RPS[bo], RT[4]], [rs("yc%d" % ti)])
                    jobs.append(dict(q=QT[:, qs, hq, 0:n], chunks=chunks, scale=0.125,
                                     acc=(PS[bo][:, 0:n], RPS[bo]), den=None, reads=RKC + [RVC, RQT[qs]], fin=fin))
                run_attn(jobs, n, hooks=hk)
            P.barrier()

        def phase_merge(b, l, tiles):
            Mt = MW[:, 0:8 * T].rearrange("p (c t) -> p c t", c=8)
            RM = [P.res("m%d" % i) for i in range(NT)]
            mcnt = [0]
            icnt = [0]
            for mc in range(8):
                wts = []
                for n_ in range(3):
                    i = stg_i[0] % NSTG
                    stg_i[0] += 1
                    vg = STG[:, i, 0:1024].rearrange("p (k n) -> p k n", k=8)
                    vb = STG[:, i, 1024:1536].rearrange("p (k n) -> p k n", k=4)
                    dma("pool", vg, wview(w_g_d[l], n_ * D + mc * 128, 128), RSTG[i], [], [RSTG[i]])
                    dma("pool", vb, wview(w_br_d[l, n_], mc * 128, 128), RSTG[i], [], [RSTG[i]])
                    wts.append((vg, vb, RSTG[i]))
                for ti in tiles:
                    t0, t1 = TILES[ti]
                    n = t1 - t0
                    tb0 = (icnt[0] % 2) * 3
                    icnt[0] += 1
                    for n_ in range(3):
                        vg, vb, rw = wts[n_]
                        bg_ = (mcnt[0] % 4) * 2
                        mcnt[0] += 1
                        for k in range(8):
                            mm(PS[bg_][:, 0:n], vg[:, k, :], UT[:, k, t0:t1], k == 0, k == 7, [rw, RUT[ti]], [RPS[bg_]])
                        sg = T32[:, tb0 + n_, 0:n]
                        act(sg, PS[bg_][:, 0:n], AF.Sigmoid, [RPS[bg_], RC], [RT[tb0 + n_]], bias=BG[:, l, n_ * 8 + mc:n_ * 8 + mc + 1])
                        for k in range(4):
                            mm(PS[bg_ + 1][:, 0:n], vb[:, k, :], YY[:, n_, k, t0:t1], k == 0, k == 3, [rw], [RPS[bg_ + 1]])
                        tt("dve", sg, sg, PS[bg_ + 1][:, 0:n], ALU.mult, [RT[tb0 + n_], RPS[bg_ + 1]], [RT[tb0 + n_]])
                    tt("dve", T32[:, tb0, 0:n], T32[:, tb0, 0:n], T32[:, tb0 + 1, 0:n], ALU.add, [RT[tb0], RT[tb0 + 1]], [RT[tb0]])
                    tt("dve", Mt[:, mc, t0:t1], T32[:, tb0, 0:n], T32[:, tb0 + 2, 0:n], ALU.add, [RT[tb0], RT[tb0 + 2]], [RM[ti]])
            wo = []
            for mo in range(8):
                slot, hf = mo // 2, mo % 2
                v_ = STG[:, slot, hf * 1024:(hf + 1) * 1024].rearrange("p (k n) -> p k n", k=8)
                dma("pool", v_, wview(w_o_d[l], mo * 128, 128), RSTG[slot], [], [RSTG[slot]])
                wo.append((v_, RSTG[slot]))
            SQ = QT[:, 0]
            RSQ = rs("nsq_f")
            bcnt = 0
            for ti in tiles:
                t0, t1 = TILES[ti]
                n = t1 - t0
                w = 2 if ti == 0 else b
                for mo in range(8):
                    bk = bcnt % 7
                    bcnt += 1
                    for k in range(8):
                        mm(PS[bk][:, 0:n], wo[mo][0][:, k, :], Mt[:, k, t0:t1], k == 0, k == 7, [wo[mo][1], RM[ti]], [RPS[bk]])
                    ht = T32[:, mo, 0:n]
                    dma("sp", ht, hd[b][mo * 128:(mo + 1) * 128, t0:t1], RT[mo], [RHD[b][ti][mo]], [RT[mo]])
                    stt("dve", ht, PS[bk][:, 0:n], scal(l, "G1", mo, w), ht, ALU.mult, ALU.add, [RPS[bk], RT[mo], rs("mod")], [RT[mo]])
                    dma("act", hd[b][mo * 128:(mo + 1) * 128, t0:t1], ht, RT[mo], [RT[mo]], [RHD[b][ti][mo]])
                tt("dve", SQ[:, 0:4, 0:n], T32[:, 0:4, 0:n], T32[:, 0:4, 0:n], ALU.mult, RT[0:4], [RSQ])
                tt("pool", SQ[:, 4:8, 0:n], T32[:, 4:8, 0:n], T32[:, 4:8, 0:n], ALU.mult, RT[4:8], [RSQ])
                for c in range(8):
                    mm(PS[7][:, 0:n], ONES[:], SQ[:, c, 0:n], c == 0, c == 7, [RSQ, RC], [RPS[7]])
                rstd_from(PS[7][:, 0:n], RSTD[:, 0:n], D, [RPS[7]], [RRS])
                for c in range(8):
                    tt("dve", T32[:, c, 0:n], T32[:, c, 0:n], RSTD[:, 0:n], ALU.mult, [RT[c], RRS], [RT[c]])
                    act(UT[:, c, t0:t1], T32[:, c, 0:n], AF.Identity, [RT[c], rs("aa"), rs("mod")], [RUT[ti]],
                        bias=scal(l, "B2", c, w), scale=scal(l, "A2", c, w))
            P.barrier()

        def phase_ffn(b, l, tiles):
            segs = ([(0, NCTX)] if 0 in tiles else []) + [(NCTX, T)]
            lo = segs[0][0]
            NH = NFC // 2
            Gt = YY[:].rearrange("p a c t -> p (a c t)")[:, 0:NH * T].rearrange("p (c t) -> p c t", c=NH)
            AROW = MW[:, 0:4 * T].bitcast(F32).rearrange("p (a t) -> p a t", a=2)
            CROW = MW[:, 4 * T:8 * T].bitcast(F32).rearrange("p (a t) -> p a t", a=2)
            RG = rs("gff")
            fcnt = [0]
            for half in range(2):
                for fi in range(NH):
                    f = half * NH + fi
                    wgt, rg_ = stage(wview(w_up_d[l], f * 128, 128), 8, 128)
                    wvl, rv_ = stage(wview(w_up_d[l], DFF + f * 128, 128), 8, 128)
                    for which, (wt, rw) in enumerate(((wgt, rg_), (wvl, rv_))):
                        for ti in tiles:
                            t0, t1 = TILES[ti]
                            n = t1 - t0
                            bk = fcnt[0] % 8
                            fcnt[0] += 1
                            for k in range(8):
                                mm(PS[bk][:, 0:n], wt[:, k, :], UT[:, k, t0:t1], k == 0, k == 7, [rw, RUT[ti]], [RPS[bk]])
                            cp("act", AROW[:, which, t0:t1], PS[bk][:, 0:n], [RPS[bk]], [rs("arow%d" % which)])
                    for which, ch in ((0, f), (1, NFC + f)):
                        src = AROW[:, which, :]
                        rsrc = rs("arow%d" % which)
                        cv = CROW[:, which, :]
                        rcv = rs("crow%d" % which)
                        eng = "dve"
                        for (s0, s1) in segs:
                            act(cv[:, s0:s1], src[:, s0:s1], AF.Identity, [rsrc, RC], [rcv],
                                bias=CB[:, l, ch:ch + 1], scale=CW[:, l, 1, ch:ch + 1])
                            stt(eng, cv[:, s0 + 1:s1], src[:, s0:s1 - 1], CW[:, l, 0, ch:ch + 1], cv[:, s0 + 1:s1], ALU.mult, ALU.add,
                                [rsrc, rcv, RC], [rcv])
                            stt(eng, cv[:, s0:s1 - 1], src[:, s0 + 1:s1], CW[:, l, 2, ch:ch + 1], cv[:, s0:s1 - 1], ALU.mult, ALU.add,
                                [rsrc, rcv, RC], [rcv])
                    act(CROW[:, 0, lo:T], CROW[:, 0, lo:T], AF.Silu, [rs("crow0")], [rs("crow0")])
                    tt("dve", Gt[:, fi, lo:T], CROW[:, 0, lo:T], CROW[:, 1, lo:T], ALU.mult, [rs("crow0"), rs("crow1")], [RG])
                for mo in range(8):
                    wd, rd_ = stage(w_dn_d[l][half * NH * 128:(half + 1) * NH * 128, mo * 128:(mo + 1) * 128].rearrange(
                        "(k p) n -> p k n", p=128), NH, 128)
                    for ti in tiles:
                        t0, t1 = TILES[ti]
                        n = t1 - t0
                        w = 2 if ti == 0 else b
                        bk = (mo * 5 + ti) % 8
                        hs = (mo * 5 + ti) % 8
                        for k in range(NH):
                            mm(PS[bk][:, 0:n], wd[:, k, :], Gt[:, k, t0:t1], k == 0, k == NH - 1, [rd_, RG], [RPS[bk]])
                        ht = T32[:, hs, 0:n]
                        dma("sp", ht, hd[b][mo * 128:(mo + 1) * 128, t0:t1], RT[hs], [RHD[b][ti][mo]], [RT[hs]])
                        stt("dve", ht, PS[bk][:, 0:n], scal(l, "G2", mo, w), ht, ALU.mult, ALU.add, [RPS[bk], RT[hs], rs("mod")], [RT[hs]])
                        dma("act", hd[b][mo * 128:(mo + 1) * 128, t0:t1], ht, RT[hs], [RT[hs]], [RHD[b][ti][mo]])
            P.barrier()

        out_toks = []

        def phase_final(b):
            HTB = [T32, MW[:, 0:8192].bitcast(F32).rearrange("p (c n) -> p c n", c=8)]
            SQS = STG[:].rearrange("p a n -> p (a n)").rearrange("p (s c n) -> p s c n", s=2, c=8)
            OT = MW[:, 8192:12288].bitcast(F32).rearrange("p (a n) -> p a n", a=2)
            ROT = [rs("ot0"), rs("ot1")]
            RSQ = [rs("fsq0"), rs("fsq1")]
            RHC = [[rs("fht%d_%d" % (s_, c)) for c in range(8)] for s_ in range(2)]
            RRF = [rs("frs0"), rs("frs1")]
            cnt = [0]
            ftiles = list(range(1, NT))

            def head(i):
                ti = ftiles[i]
                t0, t1 = TILES[ti]
                n = t1 - t0
                s_ = i % 2
                HT = HTB[s_]
                SQ = SQS[:, s_]
                dma("sp", HT[:, :, 0:n], hdv[b][:, :, t0:t1], RHC[s_][0], RHD[b][ti], RHC[s_])
                tt("dve", SQ[:, 0:4, 0:n], HT[:, 0:4, 0:n], HT[:, 0:4, 0:n], ALU.mult, RHC[s_][0:4], [RSQ[s_]])
                tt("pool", SQ[:, 4:8, 0:n], HT[:, 4:8, 0:n], HT[:, 4:8, 0:n], ALU.mult, RHC[s_][4:8], [RSQ[s_]])

            def tail(i):
                ti = ftiles[i]
                t0, t1 = TILES[ti]
                n = t1 - t0
                s_ = i % 2
                HT = HTB[s_]
                SQ = SQS[:, s_]
                RS_ = RSTDF[:, s_, 0:n]
                for c in range(8):
                    mm(PS[s_][:, 0:n], ONES[:], SQ[:, c, 0:n], c == 0, c == 7, [RSQ[s_], RC], [RPS[s_]])
                rstd_from(PS[s_][:, 0:n], RS_, D, [RPS[s_]], [RRF[s_]])
                for c in range(8):
                    stt("dve", HT[:, c, 0:n], HT[:, c, 0:n], GFS[:, c:c + 1], RS_, ALU.mult, ALU.mult,
                        [RHC[s_][c], RRF[s_], rs("lamt")], [RHC[s_][c]])
                for tb in range(n // 128):
                    slot = cnt[0] % 2
                    cnt[0] += 1
                    for half in range(2):
                        pb = PS[2 + slot * 2 + half]
                        for cc in range(4):
                            c = half * 4 + cc
                            P.op("pe", lambda e, pb=pb, cc=cc, c=c, tb=tb, HT=HT: e.transpose(
                                pb[:, cc * 128:(cc + 1) * 128], HT[:, c, tb * 128:(tb + 1) * 128], IDF[:]),
                                reads=[RHC[s_][c], RC], writes=[RPS[2 + slot * 2 + half]])
                        cp("dve" if half == 0 else "act", OT[:, slot, half * 512:(half + 1) * 512], pb[:],
                           [RPS[2 + slot * 2 + half]], [ROT[slot]])
                    tok0 = t0 - NCTX + tb * 128
                    tk = dma("sp", out_d[b, tok0:tok0 + 128, :], OT[:, slot, :], ROT[slot], [ROT[slot]], [])
                    out_toks.append(tk)

            head(0)
            for i in range(len(ftiles)):
                if i + 1 < len(ftiles):
                    head(i + 1)
                tail(i)
            P.barrier()

        def dump(name, src_ap):
            if name in dbg_out:
                P.barrier()
                t_ = P.op("pool", lambda e: e.dma_start(out=dbg_out[name], in_=src_ap), dma=rs("dbgchan_" + name))
                out_toks.append(t_)
                P.barrier()

        def dump_h(name, b):
            if name in dbg_out:
                P.barrier()
                out_toks.append(P.op("sp", lambda e: e.dma_start(out=dbg_out[name], in_=hd[b]), dma=rs("dbgchan_" + name)))
                P.barrier()

        phase_in(0)
        for pc in range(24):
            mod_piece(0, pc)
        mod_finish(0)
        P.barrier()
        mod1 = []
        if nl > 1:
            for pc in range(0, 24, 2):
                mod1.append(lambda pc=pc: (mod_piece(1, pc), mod_piece(1, pc + 1)))
        for b in range(nb):
            if b > 0:
                phase_in(b)
            for l in range(nl):
                all_t = list(range(NT))
                qt = all_t if l < L - 1 else all_t[1:]
                phase_norm(b, l, 1, all_t)
                if b == 0 and l == 0:
                    dump("ut", UT[:].rearrange("p c t -> p (c t)"))
                if b == 0 and l == 0 and mod1:
                    mixer_A(b, l, qt, extra=mod1)
                    for f_ in mod1:
                        f_()
                    mod_finish(1)
                else:
                    mixer_A(b, l, qt)
                if b == 0 and l == 0:
                    dump("ya", YY[:, 0].rearrange("p c t -> p (c t)"))
                mixer_B(b, l, qt)
                if b == 0 and l == 0:
                    dump("yb", YY[:, 1].rearrange("p c t -> p (c t)"))
                mixer_C(b, l, qt)
                if b == 0 and l == 0:
                    dump("yc", YY[:, 2].rearrange("p c t -> p (c t)"))
                phase_merge(b, l, qt)
                if b == 0 and l == 0:
                    dump_h("h1", b)
                phase_ffn(b, l, qt)
                if b == 0 and l == 0:
                    dump_h("h2", b)
            if nl == L:
                phase_final(b)
        P.streams["sp"].append((None, [(t.key, t.val) for t in out_toks], None))
        print("ops", P.n_ops, "waits", P.n_waits, "chans", len(P.chan_cnt), flush=True)
        P.emit()
    return nc


def _prep_shared(inp):
    f = np.float32
    w_in = inp["w_in"]
    Lr = w_in.shape[0]

    def swap64(a):
        s = a.reshape(a.shape[:-1] + (a.shape[-1] // 64, 2, 32))
        return s[..., ::-1, :].reshape(a.shape)

    qa, ka, va = w_in[..., 0:512], w_in[..., 512:1024], w_in[..., 1024:1536]
    cq, ckv, kr = w_in[..., 1536:1792], w_in[..., 1792:1920], w_in[..., 1920:1952]
    qc, kc, vc = w_in[..., 1952:2464], w_in[..., 2464:2592], w_in[..., 2592:2720]
    krp = np.zeros((Lr, D, 128), f)
    krp[..., 64:96] = kr
    krs = np.zeros((Lr, D, 128), f)
    krs[..., 64:80] = kr[..., 16:32]
    krs[..., 80:96] = kr[..., 0:16]
    qcp = np.concatenate([np.concatenate([qc[..., c * 64:(c + 1) * 64], qc[..., (c + 4) * 64:(c + 5) * 64]], axis=-1)
                          for c in range(4)], axis=-1)
    kcd = np.concatenate([kc, kc], axis=-1)
    w_in_ext = np.concatenate([qa, swap64(qa), ka, swap64(ka), va, cq, ckv, krp, krs, qcp, swap64(qcp), kcd, swap64(kcd), vc],
                              axis=-1).astype(f)
    assert w_in_ext.shape[-1] == NEXT
    w_uq = inp["w_uq"].reshape(Lr, 256, 8, 96)
    uo = np.zeros((Lr, 256, 8, 128), f)
    uo[..., 0:96] = w_uq
    us = np.zeros((Lr, 256, 8, 128), f)
    us[..., 64:80] = w_uq[..., 80:96]
    us[..., 80:96] = w_uq[..., 64:80]
    w_uq_ext = np.concatenate([uo.reshape(Lr, 256, 1024), us.reshape(Lr, 256, 1024)], axis=-1)

    def pcol(v, n):
        return np.ascontiguousarray(np.moveaxis(v.reshape(v.shape[:-1] + (n, 128)), -1, 0)).astype(f)

    small = np.zeros((128, Lr, 16), f)
    small[:, :, 0] = inp["g_diff"].T
    small[:, :, 1:3] = pcol(inp["g_cq"], 2)
    small[:, :, 3] = inp["g_ckv"].T
    small[:, :, 4:12] = np.broadcast_to(inp["sink"][None], (128, Lr, 8))
    cos64, sin64, tb = _rope_tables()
    shared = dict(
        w_ada=np.ascontiguousarray(inp["w_ada"], dtype=f),
        badaT=pcol(inp["b_ada"], 48),
        gn=np.ascontiguousarray(np.stack([pcol(inp["g_norm1"], 8), pcol(inp["g_norm2"], 8)], axis=2)),
        gfinT=pcol(inp["g_final"], 8),
        w_in_ext=np.ascontiguousarray(w_in_ext),
        w_uq_ext=np.ascontiguousarray(w_uq_ext),
        w_ukv=np.ascontiguousarray(inp["w_ukv"], dtype=f),
        w_branch=np.ascontiguousarray(inp["w_branch"], dtype=f),
        w_gate=np.ascontiguousarray(inp["w_gate"], dtype=f),
        bgT=pcol(inp["b_gate"], 24),
        w_o=np.ascontiguousarray(inp["w_o"], dtype=f),
        w_up=np.ascontiguousarray(inp["w_up"], dtype=f),
        cwT=pcol(inp["conv_w"], 44),
        cbT=pcol(inp["conv_b"], 44),
        w_down=np.ascontiguousarray(inp["w_down"], dtype=f),
        small=small,
        lam_rep=np.ascontiguousarray(np.broadcast_to(inp["lam"].reshape(1, Lr, 256), (128, Lr, 256))).astype(f),
        cos64=cos64, sin64=sin64, tabB=tb, masks=_masks(), ident=np.eye(128, dtype=f),
    )
    return shared


def _core_inputs(inp, shared, core, nb=NB):
    f = np.float32
    b0 = core * nb
    cvecs = np.stack([inp["c"][b0], inp["c"][b0 + 1] if nb > 1 else inp["c"][b0], inp["c_ctx"]], axis=-1)
    cT = np.ascontiguousarray(np.moveaxis(cvecs.reshape(8, 128, 3), 1, 0)).astype(f)
    m = dict(shared)
    m["x"] = np.ascontiguousarray(inp["x"][b0:b0 + NB], dtype=f)
    m["ctx"] = np.ascontiguousarray(inp["ctx"][b0:b0 + NB], dtype=f)
    m["cT"] = cT
    return m


_NC_CACHE = {}


def kernel(**inputs):
    inp = {k: np.asarray(v) for k, v in inputs.items()}
    shared = _prep_shared(inp)
    if "nc" not in _NC_CACHE:
        _NC_CACHE["nc"] = build()
    nc = _NC_CACHE["nc"]
    in_maps = [_core_inputs(inp, shared, c) for c in range(8)]
    res = run_bass_kernel_spmd(nc, in_maps, core_ids=list(range(8)))
    out = np.concatenate([np.asarray(r["out"]) for r in res.results], axis=0)
    return out.astype(np.float32)
```

```python
import math
import contextlib
import numpy as np
import concourse.bass as bass
import concourse.mybir as mybir
from concourse.bass_utils import run_bass_kernel_spmd

F32 = mybir.dt.float32
BF16 = mybir.dt.bfloat16
AF = mybir.ActivationFunctionType
ALU = mybir.AluOpType

ENGS = ("pe", "act", "dve", "pool", "sp")


class Res:
    __slots__ = ("name", "writer", "readers", "chan", "excl")

    def __init__(self, name, excl=False):
        self.name = name
        self.writer = None
        self.readers = []
        self.chan = None
        self.excl = excl


class Tok:
    __slots__ = ("key", "val", "clock")

    def __init__(self, key, val, clock):
        self.key = key
        self.val = val
        self.clock = clock


class Prog:
    def __init__(self, nc, same_eng_sync=True):
        self.nc = nc
        self.streams = {e: [] for e in ENGS}
        self.cnt = {e: 0 for e in ENGS}
        self.seen = {e: {} for e in ENGS}
        self.last = {e: None for e in ENGS}
        self.chan_cnt = {}
        self.chan_last = {}
        self.same_eng_sync = same_eng_sync
        self.all_res = []
        self.n_waits = 0
        self.n_ops = 0

    def res(self, name, excl=False):
        r = Res(name, excl)
        self.all_res.append(r)
        return r

    def _need(self, eng, tok, need):
        if tok is None:
            return
        if (not self.same_eng_sync or eng == "pe") and tok.key == eng:
            return
        if self.seen[eng].get(tok.key, 0) >= tok.val:
            return
        cur = need.get(tok.key)
        if cur is None or cur.val < tok.val:
            need[tok.key] = tok

    def _apply(self, eng, need):
        waits = []
        seen = self.seen[eng]
        for key, tok in need.items():
            waits.append((key, tok.val))
            if seen.get(key, 0) < tok.val:
                seen[key] = tok.val
            for k2, v2 in tok.clock.items():
                if seen.get(k2, 0) < v2:
                    seen[k2] = v2
        self.n_waits += len(waits)
        return waits

    def op(self, eng, fn, reads=(), writes=(), dma=None):
        need = {}
        for r in reads:
            self._need(eng, r.writer, need)
            if r.excl:
                for t in r.readers:
                    if t.key != eng:
                        self._need(eng, t, need)
        for w in writes:
            self._need(eng, w.writer, need)
            for t in w.readers:
                self._need(eng, t, need)
        if dma is not None:
            if dma.chan is None:
                dma.chan = "c%d" % len(self.chan_cnt)
                self.chan_cnt[dma.chan] = 0
                self.chan_last[dma.chan] = None
            self._need(eng, self.chan_last[dma.chan], need)
        waits = self._apply(eng, need)
        self.n_ops += 1
        clock = dict(self.seen[eng])
        if dma is not None:
            ck = dma.chan
            self.chan_cnt[ck] += 16
            tok = Tok(ck, self.chan_cnt[ck], clock)
            self.chan_last[ck] = tok
            inc = (ck, 16)
        else:
            self.cnt[eng] += 1
            tok = Tok(eng, self.cnt[eng], clock)
            self.last[eng] = tok
            inc = (eng, 1)
        self.streams[eng].append((fn, waits, inc))
        for r in reads:
            r.readers.append(tok)
        for w in writes:
            w.writer = tok
            w.readers = []
        return tok

    def barrier(self):
        toks = [t for t in self.last.values() if t is not None]
        toks += [t for t in self.chan_last.values() if t is not None]
        for eng in ENGS:
            need = {}
            for t in toks:
                if t.key == eng and eng == "pe":
                    continue
                if self.seen[eng].get(t.key, 0) >= t.val:
                    continue
                need[t.key] = t
            waits = self._apply(eng, need)
            if waits:
                self.streams[eng].append((None, waits, None))
        for r in self.all_res:
            r.writer = None
            r.readers = []

    def emit(self):
        nc = self.nc
        keys = list(ENGS) + list(self.chan_cnt.keys())
        with contextlib.ExitStack() as st:
            sems = {}
            for k in keys:
                sems[k] = st.enter_context(nc.semaphore("s_" + k))
            block = st.enter_context(nc.Block())

            def run(stream):
                def body(e):
                    for fn, waits, inc in stream:
                        for key, val in waits:
                            e.wait_ge(sems[key], val)
                        if fn is not None:
                            fn(e).then_inc(sems[inc[0]], inc[1])
                return body

            block.tensor(run(self.streams["pe"]))
            block.scalar(run(self.streams["act"]))
            block.vector(run(self.streams["dve"]))
            block.gpsimd(run(self.streams["pool"]))
            block.sync(run(self.streams["sp"]))


D = 1024
S = 2048
NCTX = 256
T = S + NCTX
NB = 2
L = 2
TILES = [(0, 256), (256, 768), (768, 1280), (1280, 1792), (1792, 2304)]
NTC = T // 128
EPS = 1e-6
DFF = 2816
NFC = DFF // 128
MLA_SCALE = 96.0 ** -0.5

QA, QAS, KA_, KAS, VA_, CQ, CKV, KR, KRS, QC, QCS, KC, KCS, VC = (
    0, 512, 1024, 1536, 2048, 2560, 2816, 2944, 3072, 3200, 3712, 4224, 4480, 4736)
NEXT = 4864


def _rope_tables():
    def tab(rot):
        nf = rot // 4
        inv = (10000.0 ** (-np.arange(nf, dtype=np.float32) / nf)).astype(np.float32)
        rows = np.repeat(np.arange(S // 64, dtype=np.float32), 64)
        cols = np.tile(np.arange(64, dtype=np.float32), S // 64)
        ang = np.concatenate([rows[:, None] * inv, cols[:, None] * inv], axis=-1)
        return np.cos(ang).astype(np.float32), np.sin(ang).astype(np.float32)
    c64, s64 = tab(64)
    c32, s32 = tab(32)
    cos64 = np.zeros((128, S), np.float32)
    sin64 = np.zeros((128, S), np.float32)
    for p in range(128):
        cos64[p] = c64[:, p % 32]
        sin64[p] = s64[:, p % 32] * (-1.0 if (p % 64) < 32 else 1.0)
    tb = np.zeros((128, S), np.float32)
    for r in range(32):
        tb[r] = c32[:, r % 16]
        tb[32 + r] = s32[:, r % 16] * (-1.0 if r < 16 else 1.0)
    return cos64, sin64, tb


def _masks():
    j = np.arange(128)[:, None]
    i = np.arange(128)[None, :]
    prev = np.where(j >= i, 0.0, -30000.0)
    nxt = np.where(j <= i, 0.0, -30000.0)
    zero = np.zeros((128, 128))
    return np.concatenate([nxt, zero, prev], axis=1).astype(np.float32)


def build(dbg=None, nb=NB, nl=L):
    nc = bass.Bass("TRN2", target_bir_lowering=False)
    dbg = dbg or ()

    def din(name, shape):
        return nc.dram_tensor(name, list(shape), F32, kind="ExternalInput").ap()

    x_d = din("x", (NB, S, D))
    ctx_d = din("ctx", (NB, NCTX, D))
    cT_d = din("cT", (128, 8, 3))
    w_ada_d = din("w_ada", (L, D, 6 * D))
    badaT_d = din("badaT", (128, L, 48))
    gn_d = din("gn", (128, L, 2, 8))
    gfin_d = din("gfinT", (128, 8))
    w_in_d = din("w_in_ext", (L, D, NEXT))
    w_uq_d = din("w_uq_ext", (L, 256, 2048))
    w_ukv_d = din("w_ukv", (L, 128, 1024))
    w_br_d = din("w_branch", (L, 3, 512, D))
    w_g_d = din("w_gate", (L, D, 3 * D))
    bgT_d = din("bgT", (128, L, 24))
    w_o_d = din("w_o", (L, D, D))
    w_up_d = din("w_up", (L, D, 2 * DFF))
    cwT_d = din("cwT", (128, L, 3, 44))
    cbT_d = din("cbT", (128, L, 44))
    w_dn_d = din("w_down", (L, DFF, D))
    small_d = din("small", (128, L, 16))
    lam_d = din("lam_rep", (128, L, 256))
    cos_d = din("cos64", (128, S))
    sin_d = din("sin64", (128, S))
    tb_d = din("tabB", (128, S))
    mask_d = din("masks", (128, 384))
    ident_d = din("ident", (128, 128))
    out_d = nc.dram_tensor("out", [NB, S, D], F32, kind="ExternalOutput").ap()
    hd = nc.dram_tensor("hd", [NB, D, T], F32).ap()
    dbg_out = {}
    for name, shape in dbg:
        dbg_out[name] = nc.dram_tensor("dbg_" + name, list(shape), F32, kind="ExternalOutput").ap()

    P = Prog(nc)
    with contextlib.ExitStack() as st:
        def sb(name, shape, dt):
            return st.enter_context(nc.sbuf_tensor(name, list(shape), dt))

        UT = sb("UT", (128, 8, T), BF16)
        YY = sb("YY", (128, 3, 4, T), BF16)
        MW = sb("MW", (128, 18432), BF16)
        SQB = sb("SQB", (128, 512), BF16)
        CQN = sb("CQN", (128, 2, 512), BF16)
        WUKV = sb("WUKV", (128, 1024), BF16)
        QT = sb("QT", (128, 2, 8, 512), BF16)
        PT = sb("PT", (128, 4, 512), BF16)
        NSTG = 4
        STG = sb("STG", (128, NSTG, 2048), BF16)
        T32 = sb("T32", (128, 8, 512), F32)
        COS = sb("COS", (128, S), BF16)
        SIN = sb("SIN", (128, S), BF16)
        TB = sb("TB", (128, S), BF16)
        MASK = sb("MASK", (128, 384), BF16)
        ONES = sb("ONES", (128, 128), BF16)
        IDB = sb("IDB", (128, 128), BF16)
        IDF = sb("IDF", (128, 128), F32)
        GFT = sb("GFT", (128, 8), F32)
        MOD = sb("MOD", (128, L, 48, 3), F32)
        AA = sb("AA", (128, L, 2, 8, 3), F32)
        BADA = sb("BADA", (128, L, 48), F32)
        GN = sb("GN", (128, L, 2, 8), F32)
        BG = sb("BG", (128, L, 24), F32)
        CW = sb("CW", (128, L, 3, 44), F32)
        CB = sb("CB", (128, L, 44), F32)
        SMALL = sb("SMALL", (128, L, 16), F32)
        LAMT = sb("LAMT", (128, L, 8), F32)
        ESINK = sb("ESINK", (128, L, 8), F32)
        GD2 = sb("GD2", (128, L, 1), F32)
        CTS = sb("CTS", (128, 8, 3), F32)
        SIL = sb("SIL", (128, 8, 3), BF16)
        EPS1K = sb("EPS1K", (128, 4), F32)
        PS = [st.enter_context(nc.psum_tensor("ps%d" % i, [128, 512], F32)) for i in range(8)]
        RPS = [P.res("ps%d" % i, excl=True) for i in range(8)]

        R = {}

        def rs(name):
            if name not in R:
                R[name] = P.res(name)
            return R[name]

        def mm(out, lhsT, rhs, start, stop, reads, writes):
            P.op("pe", lambda e: e.matmul(out, lhsT=lhsT, rhs=rhs, start=start, stop=stop),
                 reads=reads, writes=writes)

        def act(out, in_, func, reads, writes, bias=None, scale=None):
            kw = {}
            if bias is not None:
                kw["bias"] = bias
            if scale is not None:
                kw["scale"] = scale
            P.op("act", lambda e: e.activation(out=out, in_=in_, func=func, **kw), reads=reads, writes=writes)

        def cp(eng, out, in_, reads, writes):
            if eng == "act":
                P.op("act", lambda e: e.copy(out=out, in_=in_), reads=reads, writes=writes)
            else:
                P.op(eng, lambda e: e.tensor_copy(out=out, in_=in_), reads=reads, writes=writes)

        def tt(eng, out, in0, in1, op, reads, writes):
            P.op(eng, lambda e: e.tensor_tensor(out=out, in0=in0, in1=in1, op=op), reads=reads, writes=writes)

        def ts(eng, out, in0, s1, s2, op0, op1, reads, writes):
            if s2 is None:
                P.op(eng, lambda e: e.tensor_scalar(out=out, in0=in0, scalar1=s1, scalar2=None, op0=op0),
                     reads=reads, writes=writes)
            else:
                P.op(eng, lambda e: e.tensor_scalar(out=out, in0=in0, scalar1=s1, scalar2=s2, op0=op0, op1=op1),
                     reads=reads, writes=writes)

        def stt(eng, out, in0, scalar, in1, op0, op1, reads, writes):
            P.op(eng, lambda e: e.scalar_tensor_tensor(out=out, in0=in0, scalar=scalar, in1=in1, op0=op0, op1=op1),
                 reads=reads, writes=writes)

        def dma(q, out, in_, chan, reads, writes):
            return P.op(q, lambda e: e.dma_start(out=out, in_=in_), reads=reads, writes=writes, dma=chan)

        stg_i = [0]
        RSTG = [P.res("stg%d" % i) for i in range(NSTG)]

        def stage(src, kc, ncol):
            i = stg_i[0] % NSTG
            stg_i[0] += 1
            view = STG[:, i, 0:kc * ncol].rearrange("p (k n) -> p k n", k=kc)
            dma("pool", view, src, RSTG[i], [], [RSTG[i]])
            return view, RSTG[i]

        def wview(w2d, c0, ncol):
            return w2d[:, c0:c0 + ncol].rearrange("(k p) n -> p k n", p=128)

        def rstd_from(ps_ap, out_ap, n_feat, reads, writes):
            col = {D: 0, 128: 1, 256: 2}[n_feat]
            act(out_ap, ps_ap, AF.Ln, list(reads) + [RC], writes, bias=EPS1K[:, col:col + 1])
            act(out_ap, out_ap, AF.Exp, writes, writes, scale=-0.5)

        RC = rs("consts")
        for ci, (dst, src) in enumerate(((COS, cos_d), (SIN, sin_d), (TB, tb_d), (IDB, ident_d))):
            dma("pool", dst[:], src[:, :], rs("c_%d" % ci), [], [RC])
        dma("pool", MASK[:], mask_d[:, :], rs("c_mask"), [], [RC])
        LAMR = T32[:, 1, :].rearrange("p (l n) -> p l n", l=L)
        for dst, src in ((IDF, ident_d), (GFT, gfin_d)):
            dma("sp", dst[:], src[:, :], rs("c2_" + str(id(dst))), [], [RC])
        dma("sp", LAMR, lam_d[:, :, :], rs("c3_lam"), [], [RC])
        for dst, src in ((BADA, badaT_d), (BG, bgT_d), (CB, cbT_d), (SMALL, small_d), (CTS, cT_d)):
            dma("sp", dst[:], src[:, :, :], rs("c3_" + str(id(dst))), [], [RC])
        for dst, src in ((GN, gn_d), (CW, cwT_d)):
            dma("sp", dst[:], src[:, :, :, :], rs("c4_" + str(id(dst))), [], [RC])
        P.op("dve", lambda e: e.memset(ONES[:], 1.0), writes=[RC])
        P.op("dve", lambda e: e.memset(EPS1K[:, 0:1], 0.001024), writes=[RC])
        P.op("dve", lambda e: e.memset(EPS1K[:, 1:2], 0.000128), writes=[RC])
        P.op("dve", lambda e: e.memset(EPS1K[:, 2:3], 0.000256), writes=[RC])
        P.op("dve", lambda e: e.memset(EPS1K[:, 3:4], 0.0), writes=[RC])
        P.barrier()

        lam_init = [0.8 - 0.6 * math.exp(-0.3 * l) for l in range(L)]
        for l in range(L):
            tmp = T32[:, 0, 0:128]
            tt("dve", tmp[:, 0:64], LAMR[:, l, 0:64], LAMR[:, l, 64:128], ALU.mult, [RC], [rs("t0")])
            tt("dve", tmp[:, 64:128], LAMR[:, l, 128:192], LAMR[:, l, 192:256], ALU.mult, [RC], [rs("t0")])
            P.op("dve", lambda e, l=l: e.reduce_sum(out=LAMT[:, l, 0:1], in_=T32[:, 0, 0:64], axis=mybir.AxisListType.X),
                 reads=[rs("t0")], writes=[rs("lamt")])
            P.op("dve", lambda e, l=l: e.reduce_sum(out=LAMT[:, l, 1:2], in_=T32[:, 0, 64:128], axis=mybir.AxisListType.X),
                 reads=[rs("t0")], writes=[rs("lamt")])
            act(LAMT[:, l, 2:4], LAMT[:, l, 0:2], AF.Exp, [rs("lamt")], [rs("lamt")])
            tt("dve", LAMT[:, l, 4:5], LAMT[:, l, 3:4], LAMT[:, l, 2:3], ALU.subtract, [rs("lamt")], [rs("lamt")])
            ts("dve", LAMT[:, l, 7:8], LAMT[:, l, 4:5], -lam_init[l], None, ALU.add, None, [rs("lamt")], [rs("lamt")])
            act(ESINK[:, l, :], SMALL[:, l, 4:12], AF.Exp, [RC], [rs("lamt")])
            ts("dve", GD2[:, l, :], SMALL[:, l, 0:1], float((1.0 - lam_init[l]) * math.sqrt(128.0)), None, ALU.mult, None,
               [RC], [rs("lamt")])
        act(SIL[:], CTS[:], AF.Silu, [RC], [rs("sil")])
        MODR = sb("MODR", (128, L, 144), F32)

        def mod_piece(l, pc):
            wv, wr = stage(wview(w_ada_d[l], pc * 256, 256), 8, 256)
            for jj in range(2):
                for k in range(8):
                    mm(PS[6][:, jj * 3:(jj + 1) * 3], wv[:, k, jj * 128:(jj + 1) * 128], SIL[:, k, :], k == 0, k == 7,
                       [wr, rs("sil")], [RPS[6]])
            cp("dve", MODR[:, l, pc * 6:(pc + 1) * 6], PS[6][:, 0:6], [RPS[6]], [rs("modr")])

        def mod_finish(l):
            m3 = MODR[:, l, :].rearrange("p (j w) -> p j w", w=3)
            for w in range(3):
                tt("dve", MOD[:, l, :, w], m3[:, :, w], BADA[:, l, :], ALU.add, [rs("modr"), RC], [rs("mod")])
            for w in range(3):
                for which, j0 in ((0, 8), (1, 32)):
                    stt("dve", AA[:, l, which, :, w], MOD[:, l, j0:j0 + 8, w], 1.0, GN[:, l, which, :], ALU.add, ALU.mult,
                        [rs("mod"), RC], [rs("aa")])
            P.op("dve", lambda e: e.tensor_scalar(out=AAS[:, l], in0=AA[:, l], scalar1=SQRT_D, scalar2=None, op0=ALU.mult),
                 reads=[rs("aa")], writes=[rs("aa")])

        SQRT_D = math.sqrt(float(D))
        AAS = sb("AAS", (128, L, 2, 8, 3), F32)
        RSTD = sb("RSTD", (128, 512), F32)
        RSTDF = PT[:].rearrange("p a n -> p (a n)").bitcast(F32).rearrange("p (s n) -> p s n", s=2)
        GCKV2 = sb("GCKV2", (128, L, 1), F32)
        GCQ2 = sb("GCQ2", (128, L, 2), F32)
        GFS = sb("GFS", (128, 8), F32)
        P.op("dve", lambda e: e.tensor_scalar(out=GCKV2[:], in0=SMALL[:, :, 3:4], scalar1=math.sqrt(128.0), scalar2=None, op0=ALU.mult),
             reads=[RC], writes=[rs("lamt")])
        P.op("dve", lambda e: e.tensor_scalar(out=GCQ2[:], in0=SMALL[:, :, 1:3], scalar1=math.sqrt(256.0), scalar2=None, op0=ALU.mult),
             reads=[RC], writes=[rs("lamt")])
        P.op("dve", lambda e: e.tensor_scalar(out=GFS[:], in0=GFT[:], scalar1=SQRT_D, scalar2=None, op0=ALU.mult),
             reads=[RC], writes=[rs("lamt")])
        P.barrier()

        def scal(l, kind, c, w):
            if kind == "A1":
                return AAS[:, l, 0, c, w:w + 1]
            if kind == "A2":
                return AAS[:, l, 1, c, w:w + 1]
            j0 = {"B1": 0, "G1": 16, "B2": 24, "G2": 40}[kind]
            return MOD[:, l, j0 + c, w:w + 1]

        NT = len(TILES)
        hdv = [hd[b].rearrange("(c p) t -> p c t", p=128) for b in range(NB)]
        RHD = [[[P.res("hd%d_%d_%d" % (b, i, c)) for c in range(8)] for i in range(NT)] for b in range(NB)]
        RUT = [P.res("ut%d" % i) for i in range(NT)]
        RT = [P.res("t32_%d" % i) for i in range(8)]
        RPT = [P.res("pt%d" % i) for i in range(4)]
        RQT = [P.res("qt%d" % i) for i in range(2)]
        RSQB = P.res("sqb")
        RCQN = P.res("cqn")
        RRS = P.res("rstd")

        def tile_of_chunk(tc):
            t = tc * 128
            for i, (a, b_) in enumerate(TILES):
                if a <= t < b_:
                    return i
            raise ValueError

        def phase_in(b):
            XTS = MW[:, 0:8192].bitcast(F32).rearrange("p (s n) -> p s n", s=4)
            HTB = [T32, MW[:, 8192:16384].bitcast(F32).rearrange("p (c n) -> p c n", c=8)]
            RX = [rs("pin_x%d" % i) for i in range(4)]
            RH = [rs("pin_h%d" % i) for i in range(2)]
            cnt = 0
            for ti in range(NT):
                t0, t1 = TILES[ti]
                n = t1 - t0
                hb = HTB[ti % 2]
                for tb in range(n // 128):
                    tok = t0 + tb * 128
                    src = ctx_d[b, tok:tok + 128, :] if tok < NCTX else x_d[b, tok - NCTX:tok - NCTX + 128, :]
                    slot = cnt % 4
                    cnt += 1
                    xt = XTS[:, slot, :]
                    dma("sp", xt, src, RX[slot], [], [RX[slot]])
                    for half in range(2):
                        pb = PS[slot * 2 + half]
                        for cc in range(4):
                            c = half * 4 + cc
                            P.op("pe", lambda e, pb=pb, cc=cc, c=c, xt=xt: e.transpose(
                                pb[:, cc * 128:(cc + 1) * 128], xt[:, c * 128:(c + 1) * 128], IDF[:]),
                                reads=[RX[slot], RC], writes=[RPS[slot * 2 + half]])
                    cp("dve", hb[:, 0:4, tb * 128:(tb + 1) * 128], PS[slot * 2][:].rearrange("p (c n) -> p c n", c=4),
                       [RPS[slot * 2]], [RH[ti % 2]])
                    cp("act", hb[:, 4:8, tb * 128:(tb + 1) * 128], PS[slot * 2 + 1][:].rearrange("p (c n) -> p c n", c=4),
                       [RPS[slot * 2 + 1]], [RH[ti % 2]])
                dma("sp", hdv[b][:, :, t0:t1], hb[:, :, 0:n], RH[ti % 2], [RH[ti % 2]], RHD[b][ti])
            P.barrier()

        def phase_norm(b, l, which, tiles):
            kA, kB = ("A1", "B1") if which == 1 else ("A2", "B2")
            HTS = MW[:, 0:16384].bitcast(F32).rearrange("p (s c n) -> p s c n", s=2, c=8)
            SQS = STG[:].rearrange("p a n -> p (a n)").rearrange("p (s c n) -> p s c n", s=2, c=8)
            RSQ = [rs("nsq0"), rs("nsq1")]
            RHC = [[rs("nht%d_%d" % (s_, c)) for c in range(8)] for s_ in range(2)]
            RRSN = [rs("nrs0"), rs("nrs1")]
            tiles = list(tiles)

            def head(i):
                ti = tiles[i]
                t0, t1 = TILES[ti]
                n = t1 - t0
                s_ = i % 2
                HT = HTS[:, s_]
                SQ = SQS[:, s_]
                dma("sp", HT[:, :, 0:n], hdv[b][:, :, t0:t1], RHC[s_][0], RHD[b][ti], RHC[s_])
                tt("dve", SQ[:, 0:4, 0:n], HT[:, 0:4, 0:n], HT[:, 0:4, 0:n], ALU.mult, RHC[s_][0:4], [RSQ[s_]])
                tt("pool", SQ[:, 4:8, 0:n], HT[:, 4:8, 0:n], HT[:, 4:8, 0:n], ALU.mult, RHC[s_][4:8], [RSQ[s_]])

            def tail(i):
                ti = tiles[i]
                t0, t1 = TILES[ti]
                n = t1 - t0
                w = 2 if ti == 0 else b
                s_ = i % 2
                HT = HTS[:, s_]
                SQ = SQS[:, s_]
                RS_ = T32[:, s_, 0:n]
                pb = s_
                for c in range(8):
                    mm(PS[pb][:, 0:n], ONES[:], SQ[:, c, 0:n], c == 0, c == 7, [RSQ[s_], RC], [RPS[pb]])
                rstd_from(PS[pb][:, 0:n], RS_, D, [RPS[pb]], [RRSN[s_]])
                for c in range(8):
                    tt("dve", HT[:, c, 0:n], HT[:, c, 0:n], RS_, ALU.mult, [RHC[s_][c], RRSN[s_]], [RHC[s_][c]])
                    act(UT[:, c, t0:t1], HT[:, c, 0:n], AF.Identity, [RHC[s_][c], rs("aa"), rs("mod")], [RUT[ti]],
                        bias=scal(l, kB, c, w), scale=scal(l, kA, c, w))

            head(0)
            for i in range(len(tiles)):
                if i + 1 < len(tiles):
                    head(i + 1)
                tail(i)
            P.barrier()

        def run_attn(jobs, nq, LA=3, hooks=None, nsb=4):
            seq = [(j, c) for j, job in enumerate(jobs) for c in range(len(job["chunks"]))]
            hooks = list(hooks or [])
            for idx in range(len(seq) + LA):
                if idx < len(seq):
                    j, c = seq[idx]
                    job = jobs[j]
                    if c == 0 and hooks:
                        hooks.pop(0)()
                    ch = job["chunks"][c]
                    kT, v, mask = ch[0], ch[1], ch[2]
                    c0, c1 = (ch[3], ch[4]) if len(ch) > 3 else (0, nq)
                    sbk = idx % nsb
                    mm(PS[sbk][:, c0:c1], kT, job["q"][:, c0:c1], True, mask is None, job["reads"], [RPS[sbk]])
                    if mask is not None:
                        mm(PS[sbk][:, c0:c1], IDB[:], mask, False, True, [RC], [RPS[sbk]])
                    pslot = idx % 4
                    act(PT[:, pslot, c0:c1], PS[sbk][:, c0:c1], AF.Exp, [RPS[sbk]], [RPT[pslot]], scale=job["scale"])
                if idx - LA >= 0:
                    j, c = seq[idx - LA]
                    job = jobs[j]
                    ch = job["chunks"][c]
                    kT, v, mask = ch[0], ch[1], ch[2]
                    c0, c1 = (ch[3], ch[4]) if len(ch) > 3 else (0, nq)
                    nchunk = len(job["chunks"])
                    pslot = (idx - LA) % 4
                    acc_ap, acc_res = job["acc"]
                    mm(acc_ap[:, c0:c1], v, PT[:, pslot, c0:c1], c == 0, c == nchunk - 1, job["reads"] + [RPT[pslot]], [acc_res])
                    if job.get("den") is not None:
                        den_ap, den_res, d32, d32r, dbf, dbfr = job["den"]
                        if c % 3 == 0:
                            if c == 0:
                                cp("dve", d32, PT[:, pslot, 0:nq], [RPT[pslot]], [d32r])
                            else:
                                tt("dve", d32, d32, PT[:, pslot, 0:nq], ALU.add, [d32r, RPT[pslot]], [d32r])
                        else:
                            mm(den_ap, ONES[:], PT[:, pslot, 0:nq], c == 1, False, [RPT[pslot], RC], [den_res])
                        if c == nchunk - 1:
                            cp("dve", dbf, d32, [d32r], [dbfr])
                            mm(den_ap, ONES[:], dbf, False, True, [dbfr, RC], [den_res])
                    if c == nchunk - 1:
                        job["fin"]()
            for h_ in hooks:
                h_()

        def rope_evac(outs, ps_a, ps_b, cos_ap, sin_ap, rd_a, rd_b, wr, tmpi, n, add_eng="pool"):
            t1 = T32[:, tmpi, 0:n]
            t2 = T32[:, tmpi + 1, 0:n]
            tt("dve", t1, ps_a, cos_ap, ALU.mult, [rd_a, RC], [RT[tmpi]])
            tt("dve", t2, ps_b, sin_ap, ALU.mult, [rd_b, RC], [RT[tmpi + 1]])
            for rows, out_ap in outs:
                tt(add_eng, out_ap, T32[rows, tmpi, 0:n], T32[rows, tmpi + 1, 0:n], ALU.add, [RT[tmpi], RT[tmpi + 1]], wr)

        ALLROWS = slice(0, 128)

        def proj_rope(l, wcol, wcol_s, nm, dst_fn, dst_res_fn, tiles, bank0=4, npair=2, steps=None, add_eng="pool"):
            cnt = [0]

            def one(mc, wa, ra, wb_, rb, mi):
                for ti in tiles:
                    t0, t1 = TILES[ti]
                    n = t1 - t0
                    ba = bank0 + (cnt[0] % npair) * 2
                    tm = (cnt[0] % 2) * 2
                    cnt[0] += 1
                    for k in range(8):
                        mm(PS[ba][:, 0:n], wa[:, k, mi * 128:(mi + 1) * 128], UT[:, k, t0:t1], k == 0, k == 7,
                           [ra, RUT[ti]], [RPS[ba]])
                    outs = dst_fn(mc, t0, t1)
                    if ti == 0:
                        for rows, out_ap in outs:
                            cp("dve", out_ap, PS[ba][rows, 0:n], [RPS[ba]], dst_res_fn(mc, ti))
                    else:
                        for k in range(8):
                            mm(PS[ba + 1][:, 0:n], wb_[:, k, mi * 128:(mi + 1) * 128], UT[:, k, t0:t1], k == 0, k == 7,
                               [rb, RUT[ti]], [RPS[ba + 1]])
                        rope_evac(outs, PS[ba][:, 0:n], PS[ba + 1][:, 0:n],
                                  COS[:, t0 - NCTX:t1 - NCTX], SIN[:, t0 - NCTX:t1 - NCTX],
                                  RPS[ba], RPS[ba + 1], dst_res_fn(mc, ti), tm, n, add_eng=add_eng)

            for mp in range(0, nm, 2):
                nmc = min(2, nm - mp)

                def grp(mp=mp, nmc=nmc):
                    wa, ra = stage(wview(w_in_d[l], wcol + mp * 128, nmc * 128), 8, nmc * 128)
                    wb_, rb = stage(wview(w_in_d[l], wcol_s + mp * 128, nmc * 128), 8, nmc * 128)
                    return wa, ra, wb_, rb
                if steps is None:
                    wa, ra, wb_, rb = grp()
                    for mi in range(nmc):
                        one(mp + mi, wa, ra, wb_, rb, mi)
                else:
                    box = {}

                    def st0(mp=mp, box=box, grp=grp):
                        box["w"] = grp()
                        one(mp, *box["w"], 0)
                    steps.append(st0)
                    if nmc > 1:
                        steps.append(lambda mp=mp, box=box: one(mp + 1, *box["w"], 1))

        def zero_qt():
            P.op("dve", lambda e: e.memset(QT[:, 0], 0.0), writes=[RQT[0]])
            P.op("pool", lambda e: e.memset(QT[:, 1], 0.0), writes=[RQT[1]])

        LO = slice(0, 64)
        HI = slice(64, 128)

        def mixer_A(b, l, qtiles, extra=None):
            KAt = MW[:, 0:4 * T].rearrange("p (c t) -> p c t", c=4)
            VAt = MW[:, 4 * T:4 * T + NTC * 512].rearrange("p (c n) -> p c n", c=NTC)
            RKA = [P.res("ka%d" % i) for i in range(NT)]
            RVA = [P.res("va%d" % i) for i in range(NT)]
            zero_qt()
            proj_rope(l, KA_, KAS, 4, lambda mc, t0, t1: [(ALLROWS, KAt[:, mc, t0:t1])], lambda mc, ti: [RKA[ti]], range(NT))
            wv0, rv0 = stage(wview(w_in_d[l], VA_, 256), 8, 256)
            wv1, rv1 = stage(wview(w_in_d[l], VA_ + 256, 256), 8, 256)
            for tc in range(NTC):
                ti = tile_of_chunk(tc)
                bk = 4 + tc % 2
                for hf, (wv, rv) in enumerate(((wv0, rv0), (wv1, rv1))):
                    for k in range(8):
                        mm(PS[bk][:, hf * 256:(hf + 1) * 256], UT[:, k, tc * 128:(tc + 1) * 128], wv[:, k, :], k == 0, k == 7,
                           [rv, RUT[ti]], [RPS[bk]])
                cp("act" if tc % 2 else "dve", VAt[:, tc, :], PS[bk][:], [RPS[bk]], [RVA[ti]])
            kall = RKA + RVA
            DEN32 = YY[:, 2, 2:4, :].rearrange("p a n -> p (a n)")[:, 0:4096].bitcast(F32).rearrange("p (a n) -> p a n", a=4)
            DENB = CQN
            RD32 = [P.res("d32_%d" % i) for i in range(4)]
            RDB = [P.res("dbf_%d" % i) for i in range(2)]

            def qsteps(qi, ti):
                t0, t1 = TILES[ti]
                n = t1 - t0
                qs = qi % 2
                st_ = []
                proj_rope(l, QA, QAS, 4, lambda mc, a, b_, qs=qs, n=n: [(LO, QT[LO, qs, mc * 2, 0:n]), (HI, QT[HI, qs, mc * 2 + 1, 0:n])],
                          lambda mc, ti_, qs=qs: [RQT[qs]], [ti], bank0=6, npair=1, steps=st_, add_eng="dve")
                return st_

            for f_ in qsteps(0, qtiles[0]):
                f_()
            for qi, ti in enumerate(qtiles):
                t0, t1 = TILES[ti]
                n = t1 - t0
                qs = qi % 2
                hooks = qsteps(qi + 1, qtiles[qi + 1]) if qi + 1 < len(qtiles) else []
                hk = []
                hooks = hooks + [(lambda: None)] * (4 - len(hooks))
                for f_ in hooks:
                    hk += [f_, (extra.pop(0) if extra else (lambda: None))]
                kchunks = range(2) if ti == 0 else range(NTC)
                jobs = []
                for h in range(4):
                    for i in range(2):
                        jn = h * 2 + i
                        bo = 3 + jn % 2
                        jp = jn % 2

                        def fin(h=h, i=i, bo=bo, n=n, t0=t0, t1=t1, ti=ti):
                            rd = T32[:, 4, 0:n]
                            cp("dve", rd, PS[5][:, 0:n], [RPS[5]], [RT[4]])
                            act(rd, rd, AF.Ln, [RT[4]], [RT[4]])
                            act(rd, rd, AF.Exp, [RT[4]], [RT[4]], scale=-1.0)
                            on = T32[:, 5 + i, 0:n]
                            tt("dve", on, PS[bo][:, 0:n], rd, ALU.mult, [RPS[bo], RT[4]], [RT[5 + i]])
                            if i == 1:
                                o = T32[:, 7, 0:n]
                                stt("dve", o, T32[:, 6, 0:n], LAMT[:, l, 7:8], T32[:, 5, 0:n], ALU.mult, ALU.add,
                                    [RT[5], RT[6], rs("lamt")], [RT[7]])
                                sq = SQB[:, 0:n]
                                tt("pool", sq, o, o, ALU.mult, [RT[7]], [RSQB])
                                mm(PS[7][:, 0:n], ONES[:], sq, True, True, [RSQB, RC], [RPS[7]])
                                rstd_from(PS[7][:, 0:n], RSTD[:, 0:n], 128, [RPS[7]], [RRS])
                                stt("dve", YY[:, 0, h, t0:t1], o, GD2[:, l, 0:1], RSTD[:, 0:n], ALU.mult, ALU.mult,
                                    [RT[7], RRS, rs("lamt")], [rs("ya%d" % ti)])
                        jobs.append(dict(
                            q=QT[:, qs, h * 2 + i, 0:n],
                            chunks=[(KAt[:, h, kc * 128:(kc + 1) * 128], VAt[:, kc, h * 128:(h + 1) * 128], None) for kc in kchunks],
                            scale=0.125, acc=(PS[bo][:, 0:n], RPS[bo]),
                            den=(PS[5][:, 0:n], RPS[5], DEN32[:, jp, 0:n], RD32[jp], DENB[:, jp, 0:n], RDB[jp]),
                            reads=kall + [RQT[qs]], fin=fin))
                run_attn(jobs, n, LA=2, hooks=hk, nsb=3)
            P.barrier()

        def mixer_B(b, l, qtiles):
            CKVN = YY[:, 2, 0, :]
            KRt = YY[:, 2, 1, :]
            RCK = rs("ckvn")
            RKR = rs("krt")
            RWU = rs("wukv")
            dma("pool", WUKV[:], w_ukv_d[l], RWU, [], [RWU])
            wck, rck = stage(wview(w_in_d[l], CKV, 128), 8, 128)
            wkr, rkr = stage(wview(w_in_d[l], KR, 256), 8, 256)
            for ti in range(NT):
                t0, t1 = TILES[ti]
                n = t1 - t0
                for k in range(8):
                    mm(PS[4][:, 0:n], wck[:, k, :], UT[:, k, t0:t1], k == 0, k == 7, [rck, RUT[ti]], [RPS[4]])
                c1 = T32[:, 0, 0:n]
                cp("act", c1, PS[4][:, 0:n], [RPS[4]], [RT[0]])
                sq = SQB[:, 0:n]
                tt("pool", sq, c1, c1, ALU.mult, [RT[0]], [RSQB])
                mm(PS[5][:, 0:n], ONES[:], sq, True, True, [RSQB, RC], [RPS[5]])
                rstd_from(PS[5][:, 0:n], RSTD[:, 0:n], 128, [RPS[5]], [RRS])
                stt("dve", CKVN[:, t0:t1], c1, GCKV2[:, l, 0:1], RSTD[:, 0:n], ALU.mult, ALU.mult, [RT[0], RRS, rs("lamt")], [RCK])
                for k in range(8):
                    mm(PS[6][:, 0:n], wkr[:, k, 0:128], UT[:, k, t0:t1], k == 0, k == 7, [rkr, RUT[ti]], [RPS[6]])
                if ti == 0:
                    cp("act", KRt[64:96, t0:t1], PS[6][64:96, 0:n], [RPS[6]], [RKR])
                else:
                    for k in range(8):
                        mm(PS[7][:, 0:n], wkr[:, k, 128:256], UT[:, k, t0:t1], k == 0, k == 7, [rkr, RUT[ti]], [RPS[7]])
                    ta = T32[64:96, 1, 0:n]
                    tb_ = T32[64:96, 2, 0:n]
                    tt("dve", ta, PS[6][64:96, 0:n], TB[0:32, t0 - NCTX:t1 - NCTX], ALU.mult, [RPS[6], RC], [RT[1]])
                    tt("dve", tb_, PS[7][64:96, 0:n], TB[32:64, t0 - NCTX:t1 - NCTX], ALU.mult, [RPS[7], RC], [RT[2]])
                    tt("pool", KRt[64:96, t0:t1], ta, tb_, ALU.add, [RT[1], RT[2]], [RKR])
            CQNA = YY[:, 2, 2:4, :]
            RCQA = rs("cqna")
            wcq, rcq = stage(wview(w_in_d[l], CQ, 256), 8, 256)
            for ti in qtiles:
                t0, t1 = TILES[ti]
                n = t1 - t0
                for mc in range(2):
                    for k in range(8):
                        mm(PS[mc][:, 0:n], wcq[:, k, mc * 128:(mc + 1) * 128], UT[:, k, t0:t1], k == 0, k == 7,
                           [rcq, RUT[ti]], [RPS[mc]])
                    cp("act", T32[:, 4 + mc, 0:n], PS[mc][:, 0:n], [RPS[mc]], [RT[4 + mc]])
                    tt("pool", CQN[:, mc, 0:n], T32[:, 4 + mc, 0:n], T32[:, 4 + mc, 0:n], ALU.mult, [RT[4 + mc]], [RCQN])
                for mc in range(2):
                    mm(PS[2][:, 0:n], ONES[:], CQN[:, mc, 0:n], mc == 0, mc == 1, [RCQN, RC], [RPS[2]])
                rstd_from(PS[2][:, 0:n], RSTD[:, 0:n], 256, [RPS[2]], [RRS])
                for mc in range(2):
                    stt("dve", CQNA[:, mc, t0:t1], T32[:, 4 + mc, 0:n], GCQ2[:, l, mc:mc + 1], RSTD[:, 0:n], ALU.mult, ALU.mult,
                        [RT[4 + mc], RRS, rs("lamt")], [RCQA])
            P.barrier()
            KBt = MW[:, 0:4 * T].rearrange("p (c t) -> p c t", c=4)
            VBt = MW[:, 4 * T:4 * T + NTC * 512].rearrange("p (c h n) -> p c h n", c=NTC, h=4)
            for hg in range(2):
                RKB = rs("kb")
                RVB = rs("vb")
                if hg == 0:
                    P.op("pool", lambda e: e.memset(VBt[:], 1.0), writes=[RVB])
                for hh in range(4):
                    h = hg * 4 + hh
                    for ti in range(NT):
                        t0, t1 = TILES[ti]
                        n = t1 - t0
                        bk = (hh * 5 + ti) % 4
                        mm(PS[bk][0:64, 0:n], WUKV[:, h * 128:h * 128 + 64], CKVN[:, t0:t1], True, True, [RWU, RCK], [RPS[bk]])
                        cp("act" if ti % 2 else "dve", KBt[0:64, hh, t0:t1], PS[bk][0:64, 0:n], [RPS[bk]], [RKB])
                    cp("dve", KBt[64:96, hh, :], KRt[64:96, :], [RKR], [RKB])
                WV4 = WUKV[:].rearrange("p (h n) -> p h n", h=8)[:, hg * 4:(hg + 1) * 4, 64:128]
                for tc in range(NTC):
                    bk = 4 + tc % 4
                    ps4 = PS[bk][:, 0:256].rearrange("p (h n) -> p h n", h=4)
                    mm(ps4, CKVN[:, tc * 128:(tc + 1) * 128], WV4, True, True, [RWU, RCK], [RPS[bk]])
                    eng = "act" if tc % 2 else "dve"
                    cp(eng, VBt[:, tc, 0:4:2, 0:64], ps4[:, 0:4:2, :], [RPS[bk]], [RVB])
                    cp(eng, VBt[:, tc, 1:4:2, 64:128], ps4[:, 1:4:2, :], [RPS[bk]], [RVB])
                wuq, ruq = stage(w_uq_d[l][:, hg * 512:(hg + 1) * 512].rearrange("(k p) n -> p k n", p=128), 2, 512)
                wus, rus = stage(w_uq_d[l][:, 1024 + hg * 512:1024 + (hg + 1) * 512].rearrange("(k p) n -> p k n", p=128), 2, 512)

                def qsteps(qi, ti, wuq=wuq, ruq=ruq, wus=wus, rus=rus):
                    t0, t1 = TILES[ti]
                    n = t1 - t0
                    qs = qi % 2
                    st_ = []
                    for hh in range(4):
                        def one(hh=hh, t0=t0, t1=t1, n=n, qs=qs, ti=ti):
                            for mc in range(2):
                                mm(PS[6][0:96, 0:n], wuq[:, mc, hh * 128:hh * 128 + 96], CQNA[:, mc, t0:t1], mc == 0, mc == 1,
                                   [ruq, RCQA], [RPS[6]])
                            if ti == 0:
                                cp("dve", QT[0:96, qs, hh, 0:n], PS[6][0:96, 0:n], [RPS[6]], [RQT[qs]])
                            else:
                                cp("dve", QT[0:64, qs, hh, 0:n], PS[6][0:64, 0:n], [RPS[6]], [RQT[qs]])
                                for mc in range(2):
                                    mm(PS[7][0:96, 0:n], wus[:, mc, hh * 128:hh * 128 + 96], CQNA[:, mc, t0:t1], mc == 0, mc == 1,
                                       [rus, RCQA], [RPS[7]])
                                ta = T32[64:96, 2, 0:n]
                                tb_ = T32[64:96, 3, 0:n]
                                tt("dve", ta, PS[6][64:96, 0:n], TB[0:32, t0 - NCTX:t1 - NCTX], ALU.mult, [RPS[6], RC], [RT[2]])
                                tt("dve", tb_, PS[7][64:96, 0:n], TB[32:64, t0 - NCTX:t1 - NCTX], ALU.mult, [RPS[7], RC], [RT[3]])
                                tt("dve", QT[64:96, qs, hh, 0:n], ta, tb_, ALU.add, [RT[2], RT[3]], [RQT[qs]])
                        st_.append(one)
                    return st_

                for f_ in qsteps(0, qtiles[0]):
                    f_()
                for qi, ti in enumerate(qtiles):
                    t0, t1 = TILES[ti]
                    n = t1 - t0
                    qs = qi % 2
                    hooks = qsteps(qi + 1, qtiles[qi + 1]) if qi + 1 < len(qtiles) else []
                    kchunks = range(2) if ti == 0 else range(NTC)
                    jobs = []
                    for hh in range(4):
                        h = hg * 4 + hh
                        bo = 4 + hh % 2
                        even = (hh % 2 == 0)
                        orow = slice(0, 64) if even else slice(64, 128)
                        drow = slice(64, 128) if even else slice(0, 64)

                        def fin(h=h, bo=bo, orow=orow, drow=drow, n=n, t0=t0, t1=t1, ti=ti):
                            rd = T32[drow, 4, 0:n]
                            P.op("dve", lambda e: e.reciprocal(out=rd, in_=PS[bo][drow, 0:n]), reads=[RPS[bo]], writes=[RT[4]])
                            tt("dve", YY[orow, 1, h // 2, t0:t1], PS[bo][orow, 0:n], rd, ALU.mult, [RPS[bo], RT[4]], [rs("yb%d" % ti)])
                        jobs.append(dict(
                            q=QT[0:96, qs, hh, 0:n],
                            chunks=[(KBt[0:96, hh, kc * 128:(kc + 1) * 128], VBt[:, kc, hh, :], None) for kc in kchunks],
                            scale=MLA_SCALE, acc=(PS[bo][:, 0:n], RPS[bo]), den=None,
                            reads=[RKB, RVB, RQT[qs]], fin=fin))
                    run_attn(jobs, n, hooks=hooks)
                P.barrier()

        def mixer_C(b, l, qtiles):
            KCt = MW[:, 0:T].rearrange("p (c t) -> p c t", c=1)
            VCt = MW[:, 2 * T:2 * T + NTC * 384].rearrange("p (c g n) -> p c g n", c=NTC, g=2)
            RKC = [P.res("kc%d" % i) for i in range(NT)]
            RVC = rs("vc")
            zero_qt()
            proj_rope(l, KC, KCS, 1, lambda mc, t0, t1: [(ALLROWS, KCt[:, mc, t0:t1])], lambda mc, ti: [RKC[ti]], range(NT))
            P.op("pool", lambda e: e.memset(VCt[:], 1.0), writes=[RVC])
            wv, rv = stage(wview(w_in_d[l], VC, 128), 8, 128)
            for tc in range(NTC):
                ti = tile_of_chunk(tc)
                bk = 4 + tc % 2
                for k in range(8):
                    mm(PS[bk][:, 0:128], UT[:, k, tc * 128:(tc + 1) * 128], wv[:, k, :], k == 0, k == 7, [rv, RUT[ti]], [RPS[bk]])
                cp("act" if tc % 2 else "dve", VCt[:, tc, :, 64:128], PS[bk][:, 0:128].rearrange("p (g n) -> p g n", g=2),
                   [RPS[bk]], [RVC])

            def qsteps(qi, ti):
                t0, t1 = TILES[ti]
                n = t1 - t0
                qs = qi % 2
                st_ = []
                proj_rope(l, QC, QCS, 4, lambda mc, a, b_, qs=qs, n=n: [(LO, QT[LO, qs, mc, 0:n]), (HI, QT[HI, qs, mc + 4, 0:n])],
                          lambda mc, ti_, qs=qs: [RQT[qs]], [ti], bank0=6, npair=1, steps=st_, add_eng="dve")
                return st_

            for f_ in qsteps(0, qtiles[0]):
                f_()
            for qi, ti in enumerate(qtiles):
                t0, t1 = TILES[ti]
                n = t1 - t0
                qs = qi % 2
                hooks = qsteps(qi + 1, qtiles[qi + 1]) if qi + 1 < len(qtiles) else []
                hk = []
                for f_ in hooks:
                    hk += [f_, (lambda: None)]
                jobs = []
                for hq in range(8):
                    g = hq // 4
                    even = (hq % 2 == 0)
                    rows = slice(0, 64) if even else slice(64, 128)
                    drow = slice(64, 128) if even else slice(0, 64)
                    vs = slice(64, 192) if even else slice(0, 128)
                    bo = 4 + hq % 2
                    chunks = [(KCt[:, 0, kc * 128:(kc + 1) * 128], VCt[:, kc, g, vs], None) for kc in range(2)]
                    if ti > 0:
                        n0 = (ti - 1) * 4
                        for d in range(6):
                            m = n0 + d - 1
                            if m < 0 or m >= 16:
                                continue
                            kc = m + 2
                            b0 = max(0, d - 2)
                            b1 = min(3, d)
                            c0, c1 = b0 * 128, (b1 + 1) * 128
                            p0 = (2 - d + b0) * 128
                            chunks.append((KCt[:, 0, kc * 128:(kc + 1) * 128], VCt[:, kc, g, vs], MASK[:, p0:p0 + (c1 - c0)], c0, c1))

                    def fin(hq=hq, bo=bo, rows=rows, drow=drow, n=n, t0=t0, t1=t1, ti=ti):
                        rd = T32[drow, 4, 0:n]
                        act(rd, PS[bo][drow, 0:n], AF.Ln, [RPS[bo], rs("lamt")], [RT[4]], bias=ESINK[drow, l, hq:hq + 1])
                        act(rd, rd, AF.Exp, [RT[4]], [RT[4]], scale=-1.0)
                        tt("dve", YY[rows, 2, hq // 2, t0:t1], PS[bo][rows, 0:n], rd, ALU.mult, [RPS[bo], RT[4]], [rs("yc%d" % ti)])
                    jobs.append(dict(q=QT[:, qs, hq, 0:n], chunks=chunks, scale=0.125,
                                     acc=(PS[bo][:, 0:n], RPS[bo]), den=None, reads=RKC + [RVC, RQT[qs]], fin=fin))
                run_attn(jobs, n, hooks=hk)
            P.barrier()

        def phase_merge(b, l, tiles):
            Mt = MW[:, 0:8 * T].rearrange("p (c t) -> p c t", c=8)
            RM = [P.res("m%d" % i) for i in range(NT)]
            mcnt = [0]
            icnt = [0]
            for mc in range(8):
                wts = []
                for n_ in range(3):
                    i = stg_i[0] % NSTG
                    stg_i[0] += 1
                    vg = STG[:, i, 0:1024].rearrange("p (k n) -> p k n", k=8)
                    vb = STG[:, i, 1024:1536].rearrange("p (k n) -> p k n", k=4)
                    dma("pool", vg, wview(w_g_d[l], n_ * D + mc * 128, 128), RSTG[i], [], [RSTG[i]])
                    dma("pool", vb, wview(w_br_d[l, n_], mc * 128, 128), RSTG[i], [], [RSTG[i]])
                    wts.append((vg, vb, RSTG[i]))
                for ti in tiles:
                    t0, t1 = TILES[ti]
                    n = t1 - t0
                    tb0 = (icnt[0] % 2) * 3
                    icnt[0] += 1
                    for n_ in range(3):
                        vg, vb, rw = wts[n_]
                        bg_ = (mcnt[0] % 4) * 2
                        mcnt[0] += 1
                        for k in range(8):
                            mm(PS[bg_][:, 0:n], vg[:, k, :], UT[:, k, t0:t1], k == 0, k == 7, [rw, RUT[ti]], [RPS[bg_]])
                        sg = T32[:, tb0 + n_, 0:n]
                        act(sg, PS[bg_][:, 0:n], AF.Sigmoid, [RPS[bg_], RC], [RT[tb0 + n_]], bias=BG[:, l, n_ * 8 + mc:n_ * 8 + mc + 1])
                        for k in range(4):
                            mm(PS[bg_ + 1][:, 0:n], vb[:, k, :], YY[:, n_, k, t0:t1], k == 0, k == 3, [rw], [RPS[bg_ + 1]])
                        tt("dve", sg, sg, PS[bg_ + 1][:, 0:n], ALU.mult, [RT[tb0 + n_], RPS[bg_ + 1]], [RT[tb0 + n_]])
                    tt("dve", T32[:, tb0, 0:n], T32[:, tb0, 0:n], T32[:, tb0 + 1, 0:n], ALU.add, [RT[tb0], RT[tb0 + 1]], [RT[tb0]])
                    tt("dve", Mt[:, mc, t0:t1], T32[:, tb0, 0:n], T32[:, tb0 + 2, 0:n], ALU.add, [RT[tb0], RT[tb0 + 2]], [RM[ti]])
            wo = []
            for mo in range(8):
                slot, hf = mo // 2, mo % 2
                v_ = STG[:, slot, hf * 1024:(hf + 1) * 1024].rearrange("p (k n) -> p k n", k=8)
                dma("pool", v_, wview(w_o_d[l], mo * 128, 128), RSTG[slot], [], [RSTG[slot]])
                wo.append((v_, RSTG[slot]))
            SQ = QT[:, 0]
            RSQ = rs("nsq_f")
            bcnt = 0
            for ti in tiles:
                t0, t1 = TILES[ti]
                n = t1 - t0
                w = 2 if ti == 0 else b
                for mo in range(8):
                    bk = bcnt % 7
                    bcnt += 1
                    for k in range(8):
                        mm(PS[bk][:, 0:n], wo[mo][0][:, k, :], Mt[:, k, t0:t1], k == 0, k == 7, [wo[mo][1], RM[ti]], [RPS[bk]])
                    ht = T32[:, mo, 0:n]
                    dma("sp", ht, hd[b][mo * 128:(mo + 1) * 128, t0:t1], RT[mo], [RHD[b][ti][mo]], [RT[mo]])
                    stt("dve", ht, PS[bk][:, 0:n], scal(l, "G1", mo, w), ht, ALU.mult, ALU.add, [RPS[bk], RT[mo], rs("mod")], [RT[mo]])
                    dma("act", hd[b][mo * 128:(mo + 1) * 128, t0:t1], ht, RT[mo], [RT[mo]], [RHD[b][ti][mo]])
                tt("dve", SQ[:, 0:4, 0:n], T32[:, 0:4, 0:n], T32[:, 0:4, 0:n], ALU.mult, RT[0:4], [RSQ])
                tt("pool", SQ[:, 4:8, 0:n], T32[:, 4:8, 0:n], T32[:, 4:8, 0:n], ALU.mult, RT[4:8], [RSQ])
                for c in range(8):
                    mm(PS[7][:, 0:n], ONES[:], SQ[:, c, 0:n], c == 0, c == 7, [RSQ, RC], [RPS[7]])
                rstd_from(PS[7][:, 0:n], RSTD[:, 0:n], D, [RPS[7]], [RRS])
                for c in range(8):
                    tt("dve", T32[:, c, 0:n], T32[:, c, 0:n], RSTD[:, 0:n], ALU.mult, [RT[c], RRS], [RT[c]])
                    act(UT[:, c, t0:t1], T32[:, c, 0:n], AF.Identity, [RT[c], rs("aa"), rs("mod")], [RUT[ti]],
                        bias=scal(l, "B2", c, w), scale=scal(l, "A2", c, w))
            P.barrier()

        def phase_ffn(b, l, tiles):
            segs = ([(0, NCTX)] if 0 in tiles else []) + [(NCTX, T)]
            lo = segs[0][0]
            NH = NFC // 2
            Gt = YY[:].rearrange("p a c t -> p (a c t)")[:, 0:NH * T].rearrange("p (c t) -> p c t", c=NH)
            AROW = MW[:, 0:4 * T].bitcast(F32).rearrange("p (a t) -> p a t", a=2)
            CROW = MW[:, 4 * T:8 * T].bitcast(F32).rearrange("p (a t) -> p a t", a=2)
            RG = rs("gff")
            fcnt = [0]
            for half in range(2):
                for fi in range(NH):
                    f = half * NH + fi
                    wgt, rg_ = stage(wview(w_up_d[l], f * 128, 128), 8, 128)
                    wvl, rv_ = stage(wview(w_up_d[l], DFF + f * 128, 128), 8, 128)
                    for which, (wt, rw) in enumerate(((wgt, rg_), (wvl, rv_))):
                        for ti in tiles:
                            t0, t1 = TILES[ti]
                            n = t1 - t0
                            bk = fcnt[0] % 8
                            fcnt[0] += 1
                            for k in range(8):
                                mm(PS[bk][:, 0:n], wt[:, k, :], UT[:, k, t0:t1], k == 0, k == 7, [rw, RUT[ti]], [RPS[bk]])
                            cp("act", AROW[:, which, t0:t1], PS[bk][:, 0:n], [RPS[bk]], [rs("arow%d" % which)])
                    for which, ch in ((0, f), (1, NFC + f)):
                        src = AROW[:, which, :]
                        rsrc = rs("arow%d" % which)
                        cv = CROW[:, which, :]
                        rcv = rs("crow%d" % which)
                        eng = "dve"
                        for (s0, s1) in segs:
                            act(cv[:, s0:s1], src[:, s0:s1], AF.Identity, [rsrc, RC], [rcv],
                                bias=CB[:, l, ch:ch + 1], scale=CW[:, l, 1, ch:ch + 1])
                            stt(eng, cv[:, s0 + 1:s1], src[:, s0:s1 - 1], CW[:, l, 0, ch:ch + 1], cv[:, s0 + 1:s1], ALU.mult, ALU.add,
                                [rsrc, rcv, RC], [rcv])
                            stt(eng, cv[:, s0:s1 - 1], src[:, s0 + 1:s1], CW[:, l, 2, ch:ch + 1], cv[:, s0:s1 - 1], ALU.mult, ALU.add,
                                [rsrc, rcv, RC], [rcv])
                    act(CROW[:, 0, lo:T], CROW[:, 0, lo:T], AF.Silu, [rs("crow0")], [rs("crow0")])
                    tt("dve", Gt[:, fi, lo:T], CROW[:, 0, lo:T], CROW[:, 1, lo:T], ALU.mult, [rs("crow0"), rs("crow1")], [RG])
                for mo in range(8):
                    wd, rd_ = stage(w_dn_d[l][half * NH * 128:(half + 1) * NH * 128, mo * 128:(mo + 1) * 128].rearrange(
                        "(k p) n -> p k n", p=128), NH, 128)
                    for ti in tiles:
                        t0, t1 = TILES[ti]
                        n = t1 - t0
                        w = 2 if ti == 0 else b
                        bk = (mo * 5 + ti) % 8
                        hs = (mo * 5 + ti) % 8
                        for k in range(NH):
                            mm(PS[bk][:, 0:n], wd[:, k, :], Gt[:, k, t0:t1], k == 0, k == NH - 1, [rd_, RG], [RPS[bk]])
                        ht = T32[:, hs, 0:n]
                        dma("sp", ht, hd[b][mo * 128:(mo + 1) * 128, t0:t1], RT[hs], [RHD[b][ti][mo]], [RT[hs]])
                        stt("dve", ht, PS[bk][:, 0:n], scal(l, "G2", mo, w), ht, ALU.mult, ALU.add, [RPS[bk], RT[hs], rs("mod")], [RT[hs]])
                        dma("act", hd[b][mo * 128:(mo + 1) * 128, t0:t1], ht, RT[hs], [RT[hs]], [RHD[b][ti][mo]])
            P.barrier()

        out_toks = []

        def phase_final(b):
            HTB = [T32, MW[:, 0:8192].bitcast(F32).rearrange("p (c n) -> p c n", c=8)]
            SQS = STG[:].rearrange("p a n -> p (a n)").rearrange("p (s c n) -> p s c n", s=2, c=8)
            OT = MW[:, 8192:12288].bitcast(F32).rearrange("p (a n) -> p a n", a=2)
            ROT = [rs("ot0"), rs("ot1")]
            RSQ = [rs("fsq0"), rs("fsq1")]
            RHC = [[rs("fht%d_%d" % (s_, c)) for c in range(8)] for s_ in range(2)]
            RRF = [rs("frs0"), rs("frs1")]
            cnt = [0]
            ftiles = list(range(1, NT))

            def head(i):
                ti = ftiles[i]
                t0, t1 = TILES[ti]
                n = t1 - t0
                s_ = i % 2
                HT = HTB[s_]
                SQ = SQS[:, s_]
                dma("sp", HT[:, :, 0:n], hdv[b][:, :, t0:t1], RHC[s_][0], RHD[b][ti], RHC[s_])
                tt("dve", SQ[:, 0:4, 0:n], HT[:, 0:4, 0:n], HT[:, 0:4, 0:n], ALU.mult, RHC[s_][0:4], [RSQ[s_]])
                tt("pool", SQ[:, 4:8, 0:n], HT[:, 4:8, 0:n], HT[:, 4:8, 0:n], ALU.mult, RHC[s_][4:8], [RSQ[s_]])

            def tail(i):
                ti = ftiles[i]
                t0, t1 = TILES[ti]
                n = t1 - t0
                s_ = i % 2
                HT = HTB[s_]
                SQ = SQS[:, s_]
                RS_ = RSTDF[:, s_, 0:n]
                for c in range(8):
                    mm(PS[s_][:, 0:n], ONES[:], SQ[:, c, 0:n], c == 0, c == 7, [RSQ[s_], RC], [RPS[s_]])
                rstd_from(PS[s_][:, 0:n], RS_, D, [RPS[s_]], [RRF[s_]])
                for c in range(8):
                    stt("dve", HT[:, c, 0:n], HT[:, c, 0:n], GFS[:, c:c + 1], RS_, ALU.mult, ALU.mult,
                        [RHC[s_][c], RRF[s_], rs("lamt")], [RHC[s_][c]])
                for tb in range(n // 128):
                    slot = cnt[0] % 2
                    cnt[0] += 1
                    for half in range(2):
                        pb = PS[2 + slot * 2 + half]
                        for cc in range(4):
                            c = half * 4 + cc
                            P.op("pe", lambda e, pb=pb, cc=cc, c=c, tb=tb, HT=HT: e.transpose(
                                pb[:, cc * 128:(cc + 1) * 128], HT[:, c, tb * 128:(tb + 1) * 128], IDF[:]),
                                reads=[RHC[s_][c], RC], writes=[RPS[2 + slot * 2 + half]])
                        cp("dve" if half == 0 else "act", OT[:, slot, half * 512:(half + 1) * 512], pb[:],
                           [RPS[2 + slot * 2 + half]], [ROT[slot]])
                    tok0 = t0 - NCTX + tb * 128
                    tk = dma("sp", out_d[b, tok0:tok0 + 128, :], OT[:, slot, :], ROT[slot], [ROT[slot]], [])
                    out_toks.append(tk)

            head(0)
            for i in range(len(ftiles)):
                if i + 1 < len(ftiles):
                    head(i + 1)
                tail(i)
            P.barrier()

        def dump(name, src_ap):
            if name in dbg_out:
                P.barrier()
                t_ = P.op("pool", lambda e: e.dma_start(out=dbg_out[name], in_=src_ap), dma=rs("dbgchan_" + name))
                out_toks.append(t_)
                P.barrier()

        def dump_h(name, b):
            if name in dbg_out:
                P.barrier()
                out_toks.append(P.op("sp", lambda e: e.dma_start(out=dbg_out[name], in_=hd[b]), dma=rs("dbgchan_" + name)))
                P.barrier()

        phase_in(0)
        for pc in range(24):
            mod_piece(0, pc)
        mod_finish(0)
        P.barrier()
        mod1 = []
        if nl > 1:
            for pc in range(0, 24, 2):
                mod1.append(lambda pc=pc: (mod_piece(1, pc), mod_piece(1, pc + 1)))
        for b in range(nb):
            if b > 0:
                phase_in(b)
            for l in range(nl):
                all_t = list(range(NT))
                qt = all_t if l < L - 1 else all_t[1:]
                phase_norm(b, l, 1, all_t)
                if b == 0 and l == 0:
                    dump("ut", UT[:].rearrange("p c t -> p (c t)"))
                if b == 0 and l == 0 and mod1:
                    mixer_A(b, l, qt, extra=mod1)
                    for f_ in mod1:
                        f_()
                    mod_finish(1)
                else:
                    mixer_A(b, l, qt)
                if b == 0 and l == 0:
                    dump("ya", YY[:, 0].rearrange("p c t -> p (c t)"))
                mixer_B(b, l, qt)
                if b == 0 and l == 0:
                    dump("yb", YY[:, 1].rearrange("p c t -> p (c t)"))
                mixer_C(b, l, qt)
                if b == 0 and l == 0:
                    dump("yc", YY[:, 2].rearrange("p c t -> p (c t)"))
                phase_merge(b, l, qt)
                if b == 0 and l == 0:
                    dump_h("h1", b)
                phase_ffn(b, l, qt)
                if b == 0 and l == 0:
                    dump_h("h2", b)
            if nl == L:
                phase_final(b)
        P.streams["sp"].append((None, [(t.key, t.val) for t in out_toks], None))
        print("ops", P.n_ops, "waits", P.n_waits, "chans", len(P.chan_cnt), flush=True)
        P.emit()
    return nc


def _prep_shared(inp):
    f = np.float32
    w_in = inp["w_in"]
    Lr = w_in.shape[0]

    def swap64(a):
        s = a.reshape(a.shape[:-1] + (a.shape[-1] // 64, 2, 32))
        return s[..., ::-1, :].reshape(a.shape)

    qa, ka, va = w_in[..., 0:512], w_in[..., 512:1024], w_in[..., 1024:1536]
    cq, ckv, kr = w_in[..., 1536:1792], w_in[..., 1792:1920], w_in[..., 1920:1952]
    qc, kc, vc = w_in[..., 1952:2464], w_in[..., 2464:2592], w_in[..., 2592:2720]
    krp = np.zeros((Lr, D, 128), f)
    krp[..., 64:96] = kr
    krs = np.zeros((Lr, D, 128), f)
    krs[..., 64:80] = kr[..., 16:32]
    krs[..., 80:96] = kr[..., 0:16]
    qcp = np.concatenate([np.concatenate([qc[..., c * 64:(c + 1) * 64], qc[..., (c + 4) * 64:(c + 5) * 64]], axis=-1)
                          for c in range(4)], axis=-1)
    kcd = np.concatenate([kc, kc], axis=-1)
    w_in_ext = np.concatenate([qa, swap64(qa), ka, swap64(ka), va, cq, ckv, krp, krs, qcp, swap64(qcp), kcd, swap64(kcd), vc],
                              axis=-1).astype(f)
    assert w_in_ext.shape[-1] == NEXT
    w_uq = inp["w_uq"].reshape(Lr, 256, 8, 96)
    uo = np.zeros((Lr, 256, 8, 128), f)
    uo[..., 0:96] = w_uq
    us = np.zeros((Lr, 256, 8, 128), f)
    us[..., 64:80] = w_uq[..., 80:96]
    us[..., 80:96] = w_uq[..., 64:80]
    w_uq_ext = np.concatenate([uo.reshape(Lr, 256, 1024), us.reshape(Lr, 256, 1024)], axis=-1)

    def pcol(v, n):
        return np.ascontiguousarray(np.moveaxis(v.reshape(v.shape[:-1] + (n, 128)), -1, 0)).astype(f)

    small = np.zeros((128, Lr, 16), f)
    small[:, :, 0] = inp["g_diff"].T
    small[:, :, 1:3] = pcol(inp["g_cq"], 2)
    small[:, :, 3] = inp["g_ckv"].T
    small[:, :, 4:12] = np.broadcast_to(inp["sink"][None], (128, Lr, 8))
    cos64, sin64, tb = _rope_tables()
    shared = dict(
        w_ada=np.ascontiguousarray(inp["w_ada"], dtype=f),
        badaT=pcol(inp["b_ada"], 48),
        gn=np.ascontiguousarray(np.stack([pcol(inp["g_norm1"], 8), pcol(inp["g_norm2"], 8)], axis=2)),
        gfinT=pcol(inp["g_final"], 8),
        w_in_ext=np.ascontiguousarray(w_in_ext),
        w_uq_ext=np.ascontiguousarray(w_uq_ext),
        w_ukv=np.ascontiguousarray(inp["w_ukv"], dtype=f),
        w_branch=np.ascontiguousarray(inp["w_branch"], dtype=f),
        w_gate=np.ascontiguousarray(inp["w_gate"], dtype=f),
        bgT=pcol(inp["b_gate"], 24),
        w_o=np.ascontiguousarray(inp["w_o"], dtype=f),
        w_up=np.ascontiguousarray(inp["w_up"], dtype=f),
        cwT=pcol(inp["conv_w"], 44),
        cbT=pcol(inp["conv_b"], 44),
        w_down=np.ascontiguousarray(inp["w_down"], dtype=f),
        small=small,
        lam_rep=np.ascontiguousarray(np.broadcast_to(inp["lam"].reshape(1, Lr, 256), (128, Lr, 256))).astype(f),
        cos64=cos64, sin64=sin64, tabB=tb, masks=_masks(), ident=np.eye(128, dtype=f),
    )
    return shared


def _core_inputs(inp, shared, core, nb=NB):
    f = np.float32
    b0 = core * nb
    cvecs = np.stack([inp["c"][b0], inp["c"][b0 + 1] if nb > 1 else inp["c"][b0], inp["c_ctx"]], axis=-1)
    cT = np.ascontiguousarray(np.moveaxis(cvecs.reshape(8, 128, 3), 1, 0)).astype(f)
    m = dict(shared)
    m["x"] = np.ascontiguousarray(inp["x"][b0:b0 + NB], dtype=f)
    m["ctx"] = np.ascontiguousarray(inp["ctx"][b0:b0 + NB], dtype=f)
    m["cT"] = cT
    return m


_NC_CACHE = {}


def kernel(**inputs):
    inp = {k: np.asarray(v) for k, v in inputs.items()}
    shared = _prep_shared(inp)
    if "nc" not in _NC_CACHE:
        _NC_CACHE["nc"] = build()
    nc = _NC_CACHE["nc"]
    in_maps = [_core_inputs(inp, shared, c) for c in range(8)]
    res = run_bass_kernel_spmd(nc, in_maps, core_ids=list(range(8)))
    out = np.concatenate([np.asarray(r["out"]) for r in res.results], axis=0)
    return out.astype(np.float32)
```
